# Optimizing a Trainium2 kernel written in Bass

```python
import jax, jax.numpy as jnp
from jax import lax
import numpy as np

D_MODEL = 1024
BATCH = 16
SEQ = 4096
DEPTH = 4
DEC_BATCH = 8
DEC_SEQ = 16
PAST_LEN = 2048

CHUNK = 64
N_MEM = 256
D_LRU = 1024
N_LRU_BLOCKS = 8
LRU_BLOCK = D_LRU // N_LRU_BLOCKS
CONV_A_WIDTH = 4
LRU_C = 8.0
D_CONV = 512
CONV_B_WIDTH = 31
N_XATTN_HEADS = 4
XATTN_HEAD_DIM = 128
D_XATTN = N_XATTN_HEADS * XATTN_HEAD_DIM
N_BRANCHES = 3
D_FF = ((8 * D_MODEL + 3 * 256 - 1) // (3 * 256)) * 256
DEEPNORM_ALPHA = (2 * DEPTH) ** 0.25
DEEPNORM_BETA = (8 * DEPTH) ** -0.25
LN_EPS = 1e-5
COL_SPLITS = (D_LRU, 2 * D_LRU, 2 * D_LRU + 2 * D_CONV, 2 * D_LRU + 2 * D_CONV + D_XATTN)
D_IN = 2 * D_LRU + 2 * D_CONV + D_XATTN + N_BRANCHES * D_MODEL

kernel_name = "hawk_conformer_memory_streaming_step"


def layer_norm(x, g, b):
    xf = x.astype(jnp.float32)
    mu = xf.mean(-1, keepdims=True)
    var = jnp.square(xf - mu).mean(-1, keepdims=True)
    y = (xf - mu) * lax.rsqrt(var + LN_EPS)
    return (y * g.astype(jnp.float32) + b.astype(jnp.float32)).astype(x.dtype)


def causal_depthwise_conv(x_hist, w, b):
    c = w.shape[1]
    y = lax.conv_general_dilated(x_hist, w[:, None, :].astype(x_hist.dtype), window_strides=(1,), padding='VALID',
                                 dimension_numbers=('NWC', 'WIO', 'NWC'), feature_group_count=c)
    return y + b


def rg_lru(xa, h_prev, w_r, b_r, w_i, b_i, lam):
    bsz, t = xa.shape[0], xa.shape[1]
    xb = xa.reshape(bsz, t, N_LRU_BLOCKS, LRU_BLOCK)
    r = jax.nn.sigmoid(jnp.einsum('btnc,ncd->btnd', xb, w_r) + b_r).reshape(bsz, t, D_LRU)
    i = jax.nn.sigmoid(jnp.einsum('btnc,ncd->btnd', xb, w_i) + b_i).reshape(bsz, t, D_LRU)
    log_a = (LRU_C * r.astype(jnp.float32)) * jax.nn.log_sigmoid(lam.astype(jnp.float32))
    a = jnp.exp(log_a)
    u = jnp.sqrt(-jnp.expm1(2.0 * log_a)) * (i * xa).astype(jnp.float32)
    u = u.at[:, 0].add(a[:, 0] * h_prev.astype(jnp.float32))

    def combine(left, right):
        a1, b1 = left
        a2, b2 = right
        return a1 * a2, a2 * b1 + b2

    _, h = lax.associative_scan(combine, (a, u), axis=1)
    return h.astype(xa.dtype), h[:, -1].astype(xa.dtype)


def mem_cross_attention(q, k, v):
    bsz, t = q.shape[0], q.shape[1]
    s = jnp.einsum('bthd,bmhd->bhtm', q, k).astype(jnp.float32) * (XATTN_HEAD_DIM ** -0.5)
    p = jax.nn.softmax(s, axis=-1).astype(v.dtype)
    return jnp.einsum('bhtm,bmhd->bthd', p, v).reshape(bsz, t, D_XATTN)


def trunk_layer(x, mem_k, mem_v, conv_a_prev, h_prev, conv_b_prev, lw):
    (w_in, b_gate, conv_a_w, conv_a_b, lru_w_r, lru_b_r, lru_w_i, lru_b_i, lru_lambda, proj_a,
     conv_b_w, conv_b_b, ln_b_g, ln_b_b, proj_b, proj_c, w_out, ln1_g, ln1_b,
     w_ffn_gate, w_ffn_up, w_ffn_down, ln2_g, ln2_b) = lw
    bsz, t = x.shape[0], x.shape[1]
    z = x @ w_in
    xa, ya, glu, q, gates = jnp.split(z, COL_SPLITS, axis=-1)
    xa_hist = jnp.concatenate([conv_a_prev, xa], axis=1)
    xa_c = causal_depthwise_conv(xa_hist, conv_a_w, conv_a_b)
    h, h_last = rg_lru(xa_c, h_prev, lru_w_r, lru_b_r, lru_w_i, lru_b_i, lru_lambda)
    out_a = (h * jax.nn.gelu(ya)) @ proj_a
    u = glu[..., :D_CONV] * jax.nn.sigmoid(glu[..., D_CONV:])
    u_hist = jnp.concatenate([conv_b_prev, u], axis=1)
    v = causal_depthwise_conv(u_hist, conv_b_w, conv_b_b)
    out_b = jax.nn.silu(layer_norm(v, ln_b_g, ln_b_b)) @ proj_b
    out_c = mem_cross_attention(q.reshape(bsz, t, N_XATTN_HEADS, XATTN_HEAD_DIM), mem_k, mem_v) @ proj_c
    g = jax.nn.sigmoid(gates.reshape(bsz, t, N_BRANCHES, D_MODEL) + b_gate)
    merged = g[:, :, 0] * out_a + g[:, :, 1] * out_b + g[:, :, 2] * out_c
    x = layer_norm(DEEPNORM_ALPHA * x + merged @ w_out, ln1_g, ln1_b)
    ffn = (jax.nn.silu(x @ w_ffn_gate) * (x @ w_ffn_up)) @ w_ffn_down
    x = layer_norm(DEEPNORM_ALPHA * x + ffn, ln2_g, ln2_b)
    new_conv_a = xa_hist[:, -(CONV_A_WIDTH - 1):]
    new_conv_b = u_hist[:, -(CONV_B_WIDTH - 1):]
    return x, new_conv_a, h_last, new_conv_b


def setup_inputs(seed: int = 0) -> dict:
    key = jax.random.key(seed)
    ks = iter(jax.random.split(key, 48))
    nrm = lambda shape, scale: jax.random.normal(next(ks), shape, jnp.float32) * scale
    gain = lambda shape: 1.0 + nrm(shape, 0.02)
    lam_u = jax.random.uniform(next(ks), (DEPTH, D_LRU), jnp.float32, 0.9, 0.999) ** (1.0 / LRU_C)
    return {
        "x_prompt": nrm((BATCH, SEQ, D_MODEL), 1.0),
        "x_sample": nrm((DEC_BATCH, DEC_SEQ, D_MODEL), 1.0),
        "mem_prompt": nrm((BATCH, N_MEM, D_MODEL), 1.0),
        "state_conv_a": nrm((DEPTH, DEC_BATCH, CONV_A_WIDTH - 1, D_LRU), 1.0),
        "state_lru": nrm((DEPTH, DEC_BATCH, D_LRU), 0.5),
        "state_conv_b": nrm((DEPTH, DEC_BATCH, CONV_B_WIDTH - 1, D_CONV), 1.0),
        "cache_mem_k": nrm((DEPTH, DEC_BATCH, N_MEM, N_XATTN_HEADS, XATTN_HEAD_DIM), 1.0),
        "cache_mem_v": nrm((DEPTH, DEC_BATCH, N_MEM, N_XATTN_HEADS, XATTN_HEAD_DIM), 1.0),
        "ln_in_g": gain((D_MODEL,)),
        "ln_in_b": nrm((D_MODEL,), 0.02),
        "w_in": nrm((DEPTH, D_MODEL, D_IN), D_MODEL ** -0.5),
        "b_gate": nrm((DEPTH, N_BRANCHES, D_MODEL), 0.02),
        "conv_a_w": nrm((DEPTH, CONV_A_WIDTH, D_LRU), CONV_A_WIDTH ** -0.5),
        "conv_a_b": nrm((DEPTH, D_LRU), 0.02),
        "lru_w_r": nrm((DEPTH, N_LRU_BLOCKS, LRU_BLOCK, LRU_BLOCK), LRU_BLOCK ** -0.5),
        "lru_b_r": nrm((DEPTH, N_LRU_BLOCKS, LRU_BLOCK), 0.02),
        "lru_w_i": nrm((DEPTH, N_LRU_BLOCKS, LRU_BLOCK, LRU_BLOCK), LRU_BLOCK ** -0.5),
        "lru_b_i": nrm((DEPTH, N_LRU_BLOCKS, LRU_BLOCK), 0.02),
        "lru_lambda": jnp.log(lam_u) - jnp.log1p(-lam_u),
        "proj_a": nrm((DEPTH, D_LRU, D_MODEL), D_LRU ** -0.5),
        "conv_b_w": nrm((DEPTH, CONV_B_WIDTH, D_CONV), CONV_B_WIDTH ** -0.5),
        "conv_b_b": nrm((DEPTH, D_CONV), 0.02),
        "ln_b_g": gain((DEPTH, D_CONV)),
        "ln_b_b": nrm((DEPTH, D_CONV), 0.02),
        "proj_b": nrm((DEPTH, D_CONV, D_MODEL), D_CONV ** -0.5),
        "w_mem_k": nrm((DEPTH, D_MODEL, D_XATTN), D_MODEL ** -0.5),
        "w_mem_v": nrm((DEPTH, D_MODEL, D_XATTN), D_MODEL ** -0.5),
        "proj_c": nrm((DEPTH, D_XATTN, D_MODEL), D_XATTN ** -0.5),
        "w_out": nrm((DEPTH, D_MODEL, D_MODEL), DEEPNORM_BETA * D_MODEL ** -0.5),
        "ln1_g": gain((DEPTH, D_MODEL)),
        "ln1_b": nrm((DEPTH, D_MODEL), 0.02),
        "w_ffn_gate": nrm((DEPTH, D_MODEL, D_FF), D_MODEL ** -0.5),
        "w_ffn_up": nrm((DEPTH, D_MODEL, D_FF), D_MODEL ** -0.5),
        "w_ffn_down": nrm((DEPTH, D_FF, D_MODEL), DEEPNORM_BETA * D_FF ** -0.5),
        "ln2_g": gain((DEPTH, D_MODEL)),
        "ln2_b": nrm((DEPTH, D_MODEL), 0.02),
    }


def reference(x_prompt, x_sample, mem_prompt, state_conv_a, state_lru, state_conv_b, cache_mem_k, cache_mem_v,
              ln_in_g, ln_in_b, w_in, b_gate, conv_a_w, conv_a_b, lru_w_r, lru_b_r, lru_w_i, lru_b_i, lru_lambda,
              proj_a, conv_b_w, conv_b_b, ln_b_g, ln_b_b, proj_b, w_mem_k, w_mem_v, proj_c, w_out, ln1_g, ln1_b,
              w_ffn_gate, w_ffn_up, w_ffn_down, ln2_g, ln2_b):
    xp = layer_norm(x_prompt, ln_in_g, ln_in_b)
    xs = layer_norm(x_sample, ln_in_g, ln_in_b)
    bp = xp.shape[0]
    dt = xp.dtype
    zero_conv_a = jnp.zeros((bp, CONV_A_WIDTH - 1, D_LRU), dt)
    zero_h = jnp.zeros((bp, D_LRU), dt)
    zero_conv_b = jnp.zeros((bp, CONV_B_WIDTH - 1, D_CONV), dt)
    ca_p, h_p, cb_p, mk_p, mv_p = [], [], [], [], []
    ca_s, h_s, cb_s = [], [], []
    for l in range(DEPTH):
        lw = (w_in[l], b_gate[l], conv_a_w[l], conv_a_b[l], lru_w_r[l], lru_b_r[l], lru_w_i[l], lru_b_i[l],
              lru_lambda[l], proj_a[l], conv_b_w[l], conv_b_b[l], ln_b_g[l], ln_b_b[l], proj_b[l], proj_c[l],
              w_out[l], ln1_g[l], ln1_b[l], w_ffn_gate[l], w_ffn_up[l], w_ffn_down[l], ln2_g[l], ln2_b[l])
        mem_k = (mem_prompt @ w_mem_k[l]).reshape(bp, N_MEM, N_XATTN_HEADS, XATTN_HEAD_DIM)
        mem_v = (mem_prompt @ w_mem_v[l]).reshape(bp, N_MEM, N_XATTN_HEADS, XATTN_HEAD_DIM)
        xp, ca, hl, cb = trunk_layer(xp, mem_k, mem_v, zero_conv_a, zero_h, zero_conv_b, lw)
        ca_p.append(ca); h_p.append(hl); cb_p.append(cb); mk_p.append(mem_k); mv_p.append(mem_v)
        xs, ca, hl, cb = trunk_layer(xs, cache_mem_k[l], cache_mem_v[l], state_conv_a[l], state_lru[l],
                                     state_conv_b[l], lw)
        ca_s.append(ca); h_s.append(hl); cb_s.append(cb)
    new_conv_a_prompt = jnp.stack(ca_p)
    new_lru_prompt = jnp.stack(h_p)
    new_conv_b_prompt = jnp.stack(cb_p)
    new_mem_k_prompt = jnp.stack(mk_p)
    new_mem_v_prompt = jnp.stack(mv_p)
    new_conv_a_sample = jnp.stack(ca_s)
    new_lru_sample = jnp.stack(h_s)
    new_conv_b_sample = jnp.stack(cb_s)
    return (xp, xs, new_conv_a_prompt, new_lru_prompt, new_conv_b_prompt, new_mem_k_prompt, new_mem_v_prompt,
            new_conv_a_sample, new_lru_sample, new_conv_b_sample)
```

```python
import numpy as np
from collections import deque
import concourse.bass as bass
import concourse.mybir as mybir
from concourse.bass_utils import run_bass_kernel_spmd

F32 = mybir.dt.float32
BF16 = mybir.dt.bfloat16
AF = mybir.ActivationFunctionType
ALU = mybir.AluOpType

D = 1024
DEPTH = 4
D_IN = 6656
D_FF = 2816
NFF = 22
N_MEM = 256
ALPHA = float((2 * DEPTH) ** 0.25)
LN_EPS = 1e-5
SCALE = float(128 ** -0.5)
HA_OFF = 4
HB_OFF = 32

ENGS = ("pe", "act", "dve", "pool", "sp")


class Buf:
    __slots__ = ("ap", "w", "r", "name", "dsem", "psum", "dgroup")

    def __init__(self, ap, name="", psum=False, dgroup=None):
        self.ap = ap
        self.psum = psum
        self.dgroup = dgroup
        self.w = None
        self.r = {}
        self.name = name
        self.dsem = None


class FW:
    def __init__(self, nc):
        self.nc = nc
        self.prog = {e: [] for e in ENGS}
        self.count = {e: 0 for e in ENGS}
        self.seen = {e: {} for e in ENGS}
        self.sems = {}
        self.dcount = {}
        self.n_dsem = 0
        self.pool_inflight = deque()
        self.shared = set()
        self.groups = {}
        self.phase = ""
        self.labels = {e: [] for e in ENGS}
        for e in ENGS:
            self.sems[e] = nc.alloc_semaphore("sem_" + e)

    def _deps(self, eng, reads, writes):
        deps = {}

        def add(k, v):
            if deps.get(k, 0) < v:
                deps[k] = v
        for b in reads:
            if b.w is not None:
                add(*b.w)
            if b.psum:
                for k, v in b.r.items():
                    if k != eng:
                        add(k, v)
        for b in writes:
            if b.w is not None and b.w[0] != eng:
                add(*b.w)
            for k, v in b.r.items():
                if k != eng:
                    add(k, v)
        waits = []
        seen = self.seen[eng]
        for k, v in deps.items():
            if seen.get(k, 0) < v:
                seen[k] = v
                waits.append((k, v))
        return waits

    def op(self, eng, fn, reads=(), writes=(), signal=True):
        waits = self._deps(eng, reads, writes)
        if signal:
            self.count[eng] += 1
            tk = (eng, self.count[eng])
        else:
            tk = (eng, self.count[eng] + 1)
        for b in writes:
            b.w = tk
            b.r = {}
        for b in reads:
            if b.r.get(eng, 0) < tk[1]:
                b.r[eng] = tk[1]
        self.prog[eng].append((waits, fn, eng if signal else None, 1))
        self.labels[eng].append(self.phase)
        return tk

    def dsem_for(self, b):
        if b.dsem is None and b.dgroup is not None:
            if b.dgroup not in self.groups:
                k = "g_" + b.dgroup
                self.n_dsem += 1
                self.sems[k] = self.nc.alloc_semaphore("dsem_" + b.dgroup)
                self.dcount[k] = 0
                self.groups[b.dgroup] = k
                self.shared.add(k)
            b.dsem = self.groups[b.dgroup]
        if b.dsem is None:
            k = "d%d" % self.n_dsem
            self.n_dsem += 1
            self.sems[k] = self.nc.alloc_semaphore("dsem_%d" % self.n_dsem)
            self.dcount[k] = 0
            b.dsem = k
        return b.dsem

    def dma(self, q, out, in_, reads=(), writes=(), sem_buf=None, max_inflight=6):
        if sem_buf is None:
            sem_buf = (list(writes) + list(reads))[0]
        k = self.dsem_for(sem_buf)
        waits = self._deps(q, reads, writes)
        if k in self.shared and self.dcount[k] > 0 and self.seen[q].get(k, 0) < self.dcount[k]:
            self.seen[q][k] = self.dcount[k]
            waits.append((k, self.dcount[k]))
        if q == "pool":
            while len(self.pool_inflight) >= max_inflight:
                ok, ov = self.pool_inflight.popleft()
                if self.seen[q].get(ok, 0) < ov:
                    self.seen[q][ok] = ov
                    waits.append((ok, ov))
        self.dcount[k] += 16
        tk = (k, self.dcount[k])
        for b in writes:
            b.w = tk
            b.r = {}
        for b in reads:
            b.r[k] = tk[1]
        if q == "pool":
            self.pool_inflight.append(tk)
        self.prog[q].append((waits, lambda e: e.dma_start(out=out, in_=in_), k, 16))
        return tk

    def wait_all_dma(self, q):
        waits = [(k, v) for k, v in self.dcount.items() if v > 0]
        self.prog[q].append((waits, None, None, 0))

    def finish(self):
        nc = self.nc
        with nc.Block() as block:
            def mk(eng):
                def body(e):
                    for waits, fn, sk, inc in self.prog[eng]:
                        for k, v in waits:
                            e.wait_ge(self.sems[k], v)
                        if fn is not None:
                            ins = fn(e)
                            if sk is not None:
                                ins.then_inc(self.sems[sk], inc)
                return body
            block.tensor(mk("pe"))
            block.scalar(mk("act"))
            block.vector(mk("dve"))
            block.gpsimd(mk("pool"))
            block.sync(mk("sp"))


class Pool:
    def __init__(self, bufs, name):
        self.free = deque(bufs)
        self.name = name
        self.low = len(bufs)

    def get(self):
        if not self.free:
            raise RuntimeError("pool %s exhausted" % self.name)
        b = self.free.popleft()
        self.low = min(self.low, len(self.free))
        return b

    def put(self, *bs):
        for b in bs:
            self.free.append(b)


class WStream:
    def __init__(self, fw, slots, q="sp"):
        self.fw = fw
        self.free = deque(slots)
        self.pending = deque()
        self.loaded = deque()
        self.q = q

    def schedule(self, name, src_ap, per_part, src_buf):
        self.pending.append((name, src_ap, per_part, src_buf))

    def _pump(self):
        while self.free and self.pending:
            name, src_ap, per_part, src_buf = self.pending.popleft()
            slot = self.free.popleft()
            dst = slot.ap[:, :per_part]
            if len(src_ap.shape) == 3:
                dst = dst.rearrange("p (o f) -> p o f", o=src_ap.shape[1])
            self.fw.dma(self.q, dst, src_ap, reads=[src_buf], writes=[slot], sem_buf=slot)
            self.loaded.append((name, slot))

    def next(self, name):
        self._pump()
        nm, slot = self.loaded.popleft()
        assert nm == name, (nm, name)
        return slot

    def release(self, slot):
        self.free.append(slot)
        self._pump()


def const_layout():
    off = {}
    n = 0

    def add(name, w):
        nonlocal n
        off[name] = n
        n += w
    add("ln_in_g", 8)
    add("ln_in_b", 8)
    for l in range(DEPTH):
        add("b_gate%d" % l, 24)
        add("conv_a_b%d" % l, 8)
        add("lru_b_r%d" % l, 8)
        add("lru_b_i%d" % l, 8)
        add("lru_lambda%d" % l, 8)
        add("conv_b_b%d" % l, 4)
        add("ln_b_g%d" % l, 4)
        add("ln_b_b%d" % l, 4)
        add("ln1_g%d" % l, 8)
        add("ln1_b%d" % l, 8)
        add("ln2_g%d" % l, 8)
        add("ln2_b%d" % l, 8)
        add("conv_a_w%d" % l, 32)
        add("conv_b_w%d" % l, 124)
    return off, n


COFF, NCONST = const_layout()

WSHAPES = {
    "w_in_t": (DEPTH * 52 * 128, 1024),
    "proj_a_t": (DEPTH * 8 * 128, 1024),
    "proj_b_t": (DEPTH * 8 * 128, 512),
    "proj_c_t": (DEPTH * 8 * 128, 512),
    "w_out_t": (DEPTH * 8 * 128, 1024),
    "lru_w": (DEPTH * 128, 2048),
    "wk_t": (DEPTH * 4 * 128, 1024),
    "wv_m": (DEPTH * 128, 4096),
    "wg_t": (DEPTH * NFF * 128, 1024),
    "wu_t": (DEPTH * NFF * 128, 1024),
    "wd_t": (DEPTH * 8 * 128, 2816),
}


def _tile_w(w):
    L, K, Nout = w.shape
    t = w.reshape(L, K // 128, 128, Nout // 128, 128)
    t = t.transpose(0, 3, 2, 1, 4)
    return np.ascontiguousarray(t).reshape(L * (Nout // 128) * 128, K)


def _pcol(v, nch):
    v = np.asarray(v)
    lead = v.shape[:-1]
    t = v.reshape(lead + (nch, 128))
    t = np.moveaxis(t, -1, 0)
    return np.ascontiguousarray(t)


def prep_shared(inp):
    sh = {}
    sh["w_in_t"] = _tile_w(inp["w_in"])
    sh["proj_a_t"] = _tile_w(inp["proj_a"])
    sh["proj_b_t"] = _tile_w(inp["proj_b"])
    sh["proj_c_t"] = _tile_w(inp["proj_c"])
    sh["w_out_t"] = _tile_w(inp["w_out"])
    sh["wk_t"] = _tile_w(inp["w_mem_k"])
    sh["wg_t"] = _tile_w(inp["w_ffn_gate"])
    sh["wu_t"] = _tile_w(inp["w_ffn_up"])
    sh["wd_t"] = _tile_w(inp["w_ffn_down"])
    wr = np.asarray(inp["lru_w_r"]).transpose(0, 2, 1, 3)
    wi = np.asarray(inp["lru_w_i"]).transpose(0, 2, 1, 3)
    sh["lru_w"] = np.ascontiguousarray(np.stack([wr, wi], axis=2)).reshape(DEPTH * 128, 2048)
    wv = np.asarray(inp["w_mem_v"]).reshape(DEPTH, 8, 128, 512).transpose(0, 2, 1, 3)
    sh["wv_m"] = np.ascontiguousarray(wv).reshape(DEPTH * 128, 4096)
    c = np.zeros((128, NCONST), np.float32)

    def put(name, arr):
        arr = np.asarray(arr, np.float32).reshape(128, -1)
        c[:, COFF[name]:COFF[name] + arr.shape[1]] = arr
    put("ln_in_g", _pcol(inp["ln_in_g"], 8))
    put("ln_in_b", _pcol(inp["ln_in_b"], 8))
    for l in range(DEPTH):
        put("b_gate%d" % l, _pcol(inp["b_gate"][l], 8))
        put("conv_a_b%d" % l, _pcol(inp["conv_a_b"][l], 8))
        put("lru_b_r%d" % l, _pcol(np.asarray(inp["lru_b_r"][l]).reshape(-1), 8))
        put("lru_b_i%d" % l, _pcol(np.asarray(inp["lru_b_i"][l]).reshape(-1), 8))
        put("lru_lambda%d" % l, _pcol(inp["lru_lambda"][l], 8))
        put("conv_b_b%d" % l, _pcol(inp["conv_b_b"][l], 4))
        put("ln_b_g%d" % l, _pcol(inp["ln_b_g"][l], 4))
        put("ln_b_b%d" % l, _pcol(inp["ln_b_b"][l], 4))
        put("ln1_g%d" % l, _pcol(inp["ln1_g"][l], 8))
        put("ln1_b%d" % l, _pcol(inp["ln1_b"][l], 8))
        put("ln2_g%d" % l, _pcol(inp["ln2_g"][l], 8))
        put("ln2_b%d" % l, _pcol(inp["ln2_b"][l], 8))
        put("conv_a_w%d" % l, _pcol(inp["conv_a_w"][l], 8))
        cb = _pcol(inp["conv_b_w"][l], 4)
        put("conv_b_w%d" % l, np.ascontiguousarray(cb.transpose(0, 2, 1)))
    sh["consts"] = c
    return sh


def prep_core(inp, i, nseq, seq):
    m = {}
    xp = np.asarray(inp["x_prompt"])
    m["xT"] = np.ascontiguousarray(xp[nseq * i:nseq * (i + 1), :seq].transpose(0, 2, 1))
    m["xsT"] = np.ascontiguousarray(np.asarray(inp["x_sample"])[i].T)
    m["memT"] = np.ascontiguousarray(np.asarray(inp["mem_prompt"])[nseq * i:nseq * (i + 1)].transpose(0, 2, 1))
    sca = np.asarray(inp["state_conv_a"])[:, i]
    m["sca"] = np.ascontiguousarray(_pcol(sca, 8).transpose(0, 1, 3, 2))
    m["slru"] = _pcol(np.asarray(inp["state_lru"])[:, i], 8)
    scb = np.asarray(inp["state_conv_b"])[:, i]
    m["scb"] = np.ascontiguousarray(_pcol(scb, 4).transpose(0, 1, 3, 2))
    ck = np.asarray(inp["cache_mem_k"])[:, i]
    m["ckT"] = np.ascontiguousarray(ck.transpose(3, 0, 2, 1))
    cv = np.asarray(inp["cache_mem_v"])[:, i].reshape(DEPTH, 2, 128, 512)
    m["cv"] = np.ascontiguousarray(cv.transpose(2, 0, 1, 3))
    return m


class StopBuild(Exception):
    pass


class Builder:
    stop_at = None

    def chk(self, name):
        self.fw.phase = name
        if self.stop_at == name:
            raise StopBuild(name)

    def use_ctx(self, i):
        self.X, self.XB, self.X_t = self.ctx_bufs[i]
        self.cur_slot = i

    def pump(self, n=2):
        for slot, dq in self.deferred.items():
            if slot == self.cur_slot:
                continue
            while n > 0 and dq:
                dq.popleft()()
                n -= 1

    def flush(self, slot):
        dq = self.deferred[slot]
        while dq:
            dq.popleft()()

    def use_seq(self, i):
        for k, v in self.seq_bufs[i].items():
            setattr(self, k, v)

    def __init__(self, nseq, seq, depth=DEPTH, tile=512, with_sample=True, n_wslots=4, NF=23, NH=24):
        self.nseq, self.seq, self.depth, self.TN, self.with_sample = nseq, seq, depth, tile, with_sample
        nc = self.nc = bass.Bass("TRN2", target_bir_lowering=False)
        fw = self.fw = FW(nc)
        L = DEPTH
        di = lambda name, shape: nc.dram_tensor(name, list(shape), F32, kind="ExternalInput").ap()
        do = lambda name, shape: nc.dram_tensor(name, list(shape), F32, kind="ExternalOutput").ap()
        self.xT = di("xT", [nseq, D, seq])
        self.xsT = di("xsT", [D, 16])
        self.memT = di("memT", [nseq, D, N_MEM])
        self.sca = di("sca", [128, L, 8, 3])
        self.slru = di("slru", [128, L, 8])
        self.scb = di("scb", [128, L, 4, 30])
        self.ckT = di("ckT", [128, L, 4, N_MEM])
        self.cv = di("cv", [128, L, 2, 512])
        self.consts_d = di("consts", [128, NCONST])
        self.win = {k: di(k, v) for k, v in WSHAPES.items()}
        self.wsc = {k: nc.dram_tensor("sc_" + k, list(v), BF16).ap() for k, v in WSHAPES.items()}
        self.wsc_buf = {(k, l): Buf(None, "sc_%s%d" % (k, l), dgroup="SC%d" % l) for k in WSHAPES for l in range(L)}
        self.da_sc = nc.dram_tensor("sc_da", [L * 128, 4096], BF16).ap()
        self.db_sc = nc.dram_tensor("sc_db", [L * 4 * 128, 31 * 128], BF16).ap()
        self.da_buf = [Buf(None, "sc_da%d" % l, dgroup="SC%d" % l) for l in range(L)]
        self.db_buf = [Buf(None, "sc_db%d" % l, dgroup="SC%d" % l) for l in range(L)]
        self.yT = do("yT", [nseq, D, seq])
        self.ysT = do("ysT", [D, 16])
        self.o_ca = do("o_ca", [nseq + 1, L, 128, 8, 3])
        self.o_lru = do("o_lru", [nseq + 1, L, 128, 8])
        self.o_cb = do("o_cb", [nseq + 1, L, 128, 4, 30])
        self.o_kT = do("o_kT", [nseq, L, 4, 128, N_MEM])
        self.o_v = do("o_v", [nseq, L, N_MEM, 512])

        sb = lambda name, shape, dt: nc.alloc_sbuf_tensor(name, list(shape), dt)
        TN = tile
        self.ctx_bufs = []
        for i in range(2):
            xt = sb("X%d" % i, [128, 8, TN], F32)
            xbt = sb("XB%d" % i, [128, 8, TN], BF16)
            self.ctx_bufs.append(([Buf(xt[:, c, :], "X%d_%d" % (i, c), dgroup="X%d" % i) for c in range(8)],
                                  [Buf(xbt[:, c, :], "XB%d_%d" % (i, c)) for c in range(8)], xt))
        self.XAH_t = sb("XAH", [128, 8, HA_OFF + TN], BF16)
        self.XAH = [Buf(self.XAH_t[:, c, :], "XAH%d" % c) for c in range(8)]
        self.UH_t = sb("UH", [128, 4, HB_OFF + TN], BF16)
        self.UH = [Buf(self.UH_t[:, c, :], "UH%d" % c) for c in range(4)]
        self.seq_bufs = []
        for i in range(2):
            kt = sb("KT%d" % i, [128, L, 4, N_MEM], BF16)
            vv = sb("VV%d" % i, [128, L, 2, 512], BF16)
            ha = sb("HA%d" % i, [128, L, 8, 3], BF16)
            hb = sb("HB%d" % i, [128, L, 4, 30], BF16)
            hs = sb("HST%d" % i, [128, L, 8], F32)
            oa = sb("OA%d" % i, [128, L, 8, 3], F32)
            ob = sb("OB%d" % i, [128, L, 4, 30], F32)
            self.seq_bufs.append(dict(
                KT=[Buf(kt[:, l], "KT%d_%d" % (i, l), dgroup="S%d" % i) for l in range(L)],
                VV=[Buf(vv[:, l], "VV%d_%d" % (i, l), dgroup="S%d" % i) for l in range(L)],
                HA=[Buf(ha[:, l], "HA%d_%d" % (i, l), dgroup="S%d" % i) for l in range(L)],
                HB=[Buf(hb[:, l], "HB%d_%d" % (i, l), dgroup="S%d" % i) for l in range(L)],
                HST=[Buf(hs[:, l], "HST%d_%d" % (i, l), dgroup="S%d" % i) for l in range(L)],
                OA=[Buf(oa[:, l], "OA%d_%d" % (i, l), dgroup="S%d" % i) for l in range(L)],
                OB=[Buf(ob[:, l], "OB%d_%d" % (i, l), dgroup="S%d" % i) for l in range(L)]))
        ft = sb("FP", [128, NF, TN], F32)
        self.F = Pool([Buf(ft[:, i, :], "F%d" % i, dgroup="F") for i in range(NF)], "F")
        ht = sb("HP", [128, NH, TN], BF16)
        self.H = Pool([Buf(ht[:, i, :], "H%d" % i) for i in range(NH)], "H")
        wt = sb("WS", [128, n_wslots, 4096], BF16)
        wslots = [Buf(wt[:, i, :], "W%d" % i) for i in range(n_wslots)]
        self.W = WStream(fw, wslots, q="sp")
        self.DGS = wslots[:2]
        self.C = Buf(sb("C", [128, NCONST], F32)[:], "C")
        self.C8 = Buf(sb("C8", [128, L, 8], F32)[:], "C8")
        self.C16 = Buf(sb("C16", [128, L, 8], F32)[:], "C16")
        self.IDENT = Buf(sb("IDENT", [128, 128], F32)[:], "IDENT")
        self.ONES = Buf(sb("ONES", [128, 128], BF16)[:], "ONES")
        self.ONES1K = Buf(sb("ONES1K", [128, 128], BF16)[:], "ONES1K")
        self.ONES512 = Buf(sb("ONES512", [128, 128], BF16)[:], "ONES512")
        self.deferred = {0: deque(), 1: deque()}
        self.use_ctx(0)
        self.use_seq(0)
        self.PS = Pool([Buf(nc.alloc_psum_tensor("ps%d" % i, [128, 512], F32)[:], "ps%d" % i, psum=True) for i in range(8)], "PS")
        self.sbuf_left = nc.sbuf_bytes_remaining

    def cc(self, name, j=0, w=1):
        o = COFF[name] + j
        return self.C.ap[:, o:o + w]

    def mm(self, ps, out_ap, lhsT, rhs, start, stop, reads):
        self.fw.op("pe", lambda e: e.matmul(out_ap, lhsT=lhsT, rhs=rhs, start=start, stop=stop),
                   reads=reads, writes=[ps], signal=stop)

    def group(self, ps, out_ap, items):
        n = len(items)
        for i, (lt, rh, rd) in enumerate(items):
            self.mm(ps, out_ap, lt, rh, i == 0, i == n - 1, rd)

    def act(self, out_b, out_ap, in_b, in_ap, func, scale=None, bias=None, extra=()):
        kw = {}
        rd = [in_b] + list(extra)
        if scale is not None:
            kw["scale"] = scale
        if bias is not None:
            kw["bias"] = bias
        self.fw.op("act", lambda e: e.activation(out=out_ap, in_=in_ap, func=func, **kw), reads=rd, writes=[out_b])

    def tt(self, out_b, out_ap, a_b, a_ap, b_b, b_ap, op, eng="dve"):
        self.fw.op(eng, lambda e: e.tensor_tensor(out=out_ap, in0=a_ap, in1=b_ap, op=op), reads=[a_b, b_b], writes=[out_b])

    def bg_step(self):
        self.bg_tick += 1
        if self.bg_casts and self.bg_tick % 3 == 0:
            self.bg_casts.popleft()()

    def cp(self, out_b, out_ap, in_b, in_ap, eng="dve", extra_w=(), extra_r=()):
        self.fw.op(eng, lambda e: e.tensor_copy(out=out_ap, in_=in_ap), reads=[in_b] + list(extra_r),
                   writes=[out_b] + list(extra_w))
        if eng == "pool":
            self.bg_step()

    def phase0(self):
        fw, nc = self.fw, self.nc
        fw.dma("pool", self.C.ap, self.consts_d, writes=[self.C])
        self.bg_casts = deque()
        self.bg_tick = 0
        order = [("wk_t", l) for l in range(DEPTH)] + [("wv_m", l) for l in range(DEPTH)]
        order += [(k, l) for l in range(DEPTH) for k in WSHAPES if k not in ("wk_t", "wv_m")]
        for k, l in order:
            rows, Fdim = WSHAPES[k]
            per_l = rows // DEPTH
            src = self.win[k][l * per_l:(l + 1) * per_l, :]
            dst = self.wsc[k][l * per_l:(l + 1) * per_l, :]
            if Fdim > 2048:
                src = src.rearrange("r (a b) -> r a b", b=Fdim // 2)
                dst = dst.rearrange("r (a b) -> r a b", b=Fdim // 2)
            elif Fdim == 1024:
                src = src.rearrange("(r two) f -> r (two f)", two=2)
                dst = dst.rearrange("(r two) f -> r (two f)", two=2)
            th = (lambda dst=dst, src=src, k=k, l=l: fw.dma("pool", dst, src, writes=[self.wsc_buf[(k, l)]], max_inflight=1))
            if l == 0 or k in ("wk_t", "wv_m"):
                th()
            else:
                self.bg_casts.append(th)
        fw.op("dve", lambda e: e.memset(self.ONES.ap, 1.0), writes=[self.ONES])
        fw.op("dve", lambda e: e.memset(self.ONES1K.ap, 1.0 / 1024), writes=[self.ONES1K])
        fw.op("dve", lambda e: e.memset(self.ONES512.ap, 1.0 / 512), writes=[self.ONES512])
        fw.op("dve", lambda e: e.memset(self.IDENT.ap, 1.0), writes=[self.IDENT])
        fw.op("pool", lambda e: e.affine_select(out=self.IDENT.ap, in_=self.IDENT.ap, pattern=[[1, 128]],
                                                 compare_op=ALU.is_equal, fill=0.0, base=0, channel_multiplier=-1),
              reads=[self.IDENT], writes=[self.IDENT])
        for l in range(DEPTH):
            lam = self.cc("lru_lambda%d" % l, 0, 8)
            c8 = self.C8.ap[:, l, :]
            self.act(self.C8, c8, self.C, lam, AF.Sigmoid)
            self.act(self.C8, c8, self.C8, c8, AF.Ln)
            self.fw.op("dve", lambda e, c8=c8: e.tensor_scalar(out=c8, in0=c8, scalar1=8.0, scalar2=None, op0=ALU.mult),
                       reads=[self.C8], writes=[self.C8])
            c16 = self.C16.ap[:, l, :]
            self.fw.op("dve", lambda e, c8=c8, c16=c16: e.tensor_scalar(out=c16, in0=c8, scalar1=2.0, scalar2=None, op0=ALU.mult),
                       reads=[self.C8], writes=[self.C16])
        n = 0
        for l in range(self.depth):
            st = self.DGS[n % 2]; n += 1
            stv = st.ap.rearrange("p (a b) -> p a b", a=32)
            for k in range(4):
                for c in range(8):
                    col = self.cc("conv_a_w%d" % l, k * 8 + c)
                    oap = stv[:, k * 8 + c, :]
                    fw.op("dve", lambda e, oap=oap, col=col: e.tensor_scalar(out=oap, in0=self.IDENT.ap, scalar1=col, scalar2=None, op0=ALU.mult),
                          reads=[self.IDENT, self.C], writes=[st])
            fw.dma("pool", self.da_sc[l * 128:(l + 1) * 128, :], st.ap,
                   reads=[st], writes=[self.da_buf[l]], sem_buf=self.da_buf[l])
            for c in range(4):
                st = self.DGS[n % 2]; n += 1
                stv = st.ap.rearrange("p (a b) -> p a b", a=32)
                for k in range(31):
                    col = self.cc("conv_b_w%d" % l, c * 31 + k)
                    oap = stv[:, k, :]
                    fw.op("dve", lambda e, oap=oap, col=col: e.tensor_scalar(out=oap, in0=self.IDENT.ap, scalar1=col, scalar2=None, op0=ALU.mult),
                          reads=[self.IDENT, self.C], writes=[st])
                r0 = (l * 4 + c) * 128
                fw.dma("pool", self.db_sc[r0:r0 + 128, :], st.ap[:, 0:31 * 128],
                       reads=[st], writes=[self.db_buf[l]], sem_buf=self.db_buf[l])

    def _blk(self, key, l, nper, o0, no):
        r0 = (l * nper + o0) * 128
        Fdim = WSHAPES[key][1]
        ap = self.wsc[key][r0:r0 + no * 128, :].rearrange("(o p) f -> p o f", p=128)
        return ap, no * Fdim

    def sched_kv(self, l):
        W = self.W
        ap, pp = self._blk("wk_t", l, 4, 0, 4)
        W.schedule("wk%d" % l, ap, pp, self.wsc_buf[("wk_t", l)])
        W.schedule("wv%d" % l, self.wsc["wv_m"][l * 128:(l + 1) * 128, :], 4096, self.wsc_buf[("wv_m", l)])

    def sched_layer(self, l, part):
        W = self.W

        def win(s):
            ap, pp = self._blk("w_in_t", l, 52, 4 * s, 4)
            W.schedule("win%d.%d" % (l, s), ap, pp, self.wsc_buf[("w_in_t", l)])

        def tiled(key, nm, nper, o0, no):
            ap, pp = self._blk(key, l, nper, o0, no)
            W.schedule("%s%d.%d" % (nm, l, o0), ap, pp, self.wsc_buf[(key, l)])
        if part == 2:
            for s in range(6):
                no = 4 if s < 5 else 2
                tiled("wg_t", "fg", NFF, 4 * s, no)
                tiled("wu_t", "fu", NFF, 4 * s, no)
            for o in range(8):
                tiled("wd_t", "fd", 8, o, 1)
            return
        for s in range(7):
            win(s)
        W.schedule("da%d" % l, self.da_sc[l * 128:(l + 1) * 128, :], 4096, self.da_buf[l])
        W.schedule("lru%d" % l, self.wsc["lru_w"][l * 128:(l + 1) * 128, :], 2048, self.wsc_buf[("lru_w", l)])
        for c in range(4):
            r0 = (l * 4 + c) * 128
            W.schedule("db%d.%d" % (l, c), self.db_sc[r0:r0 + 128, :], 31 * 128, self.db_buf[l])
        tiled("proj_c_t", "pc", 8, 0, 8)
        for g in range(2):
            win(11 + g)
        tiled("proj_b_t", "pb", 8, 0, 8)
        for g in range(2):
            win(9 + g)
        for g in range(2):
            win(7 + g)
            tiled("proj_a_t", "pa", 8, 4 * g, 4)
        for g in range(2):
            tiled("w_out_t", "wo", 8, 4 * g, 4)

    def rounds(self):
        nt = self.seq // self.TN
        rs = []
        for s0 in range(0, self.nseq, 2):
            seqs = list(range(s0, min(s0 + 2, self.nseq)))
            for t in range(nt):
                rs.append([("p", i, s, t) for i, s in enumerate(seqs)])
        if self.with_sample:
            rs.append([("s", 0, self.nseq, 0)])
        return rs

    def schedule_all(self):
        done_kv = set()
        for rnd in self.rounds():
            for kind, slot, s, t in rnd:
                if kind == "p" and s not in done_kv:
                    done_kv.add(s)
                    for l in range(self.depth):
                        self.sched_kv(l)
            for l in range(self.depth):
                for _ in rnd:
                    self.sched_layer(l, 1)
                for _ in rnd:
                    self.sched_layer(l, 2)

    def layernorm(self, src, nch, N, ones_b, gname, bname, func, dst, post=None, defer=False, on_done=None, defer_to=None):
        F, H, PS = self.F, self.H, self.PS
        rb, rsq = [], []
        for c in range(nch):
            q = H.get(); self.act(q, q.ap[:, :N], src[c], src[c].ap[:, :N], AF.Square); rsq.append(q)
            b = H.get(); self.cp(b, b.ap[:, :N], src[c], src[c].ap[:, :N], eng="pool"); rb.append(b)
        pm = PS.get(); pq = PS.get()
        self.group(pm, pm.ap[:, :N], [(ones_b.ap, rb[c].ap[:, :N], [ones_b, rb[c]]) for c in range(nch)])
        self.group(pq, pq.ap[:, :N], [(ones_b.ap, rsq[c].ap[:, :N], [ones_b, rsq[c]]) for c in range(nch)])
        H.put(*rb); H.put(*rsq)
        t = F.get()
        th = []
        th.append(lambda: self.act(t, t.ap[:, :N], pm, pm.ap[:, :N], AF.Square))

        def _var():
            self.tt(t, t.ap[:, :N], pq, pq.ap[:, :N], t, t.ap[:, :N], ALU.subtract)
            PS.put(pq)
        th.append(_var)
        th.append(lambda: self.fw.op("dve", lambda e: e.tensor_scalar(out=t.ap[:, :N], in0=t.ap[:, :N], scalar1=0.0, scalar2=LN_EPS, op0=ALU.max, op1=ALU.add),
                                     reads=[t], writes=[t]))
        th.append(lambda: self.act(t, t.ap[:, :N], t, t.ap[:, :N], AF.Sqrt))
        th.append(lambda: self.fw.op("dve", lambda e: e.reciprocal(out=t.ap[:, :N], in_=t.ap[:, :N]), reads=[t], writes=[t]))
        for c in range(nch):
            def _n1(c=c):
                s = src[c]
                self.tt(s, s.ap[:, :N], s, s.ap[:, :N], pm, pm.ap[:, :N], ALU.subtract)
                self.tt(s, s.ap[:, :N], s, s.ap[:, :N], t, t.ap[:, :N], ALU.mult)

            def _n2(c=c):
                s = src[c]
                db, dap = dst(c)
                self.act(db, dap, s, s.ap[:, :N], func, scale=self.cc(gname, c), bias=self.cc(bname, c), extra=[self.C])
                if post is not None:
                    post(c)
            th.append(_n1)
            th.append(_n2)

        def _fin():
            PS.put(pm)
            F.put(t)
            if on_done is not None:
                on_done()
        th.append(_fin)
        if defer_to is not None:
            defer_to.extend(th)
        elif defer:
            self.deferred[self.cur_slot].extend(th)
        else:
            for f in th:
                f()

    def kv_phase(self, s):
        fw, F, H, PS, W = self.fw, self.F, self.H, self.PS, self.W
        mf = [F.get() for _ in range(4)]
        mb = [H.get() for _ in range(4)]
        for i in range(4):
            src = self.memT[s, i * 256:(i + 1) * 256, :].rearrange("(c p) m -> p c m", p=128)
            fw.dma("sp", mf[i].ap[:, :512].rearrange("p (c m) -> p c m", c=2), src, writes=[mf[i]])
            self.cp(mb[i], mb[i].ap[:, :512], mf[i], mf[i].ap[:, :512])
        F.put(*mf)
        self.chk("kv1")

        def memb(kc):
            return mb[kc // 2], mb[kc // 2].ap[:, (kc % 2) * 256:(kc % 2) * 256 + 256]
        for l in range(self.depth):
            wk = W.next("wk%d" % l)
            self.chk("kv2")
            wkv = wk.ap[:, :4096].rearrange("p (o f) -> p o f", o=4)
            for h in range(4):
                ps = PS.get()
                self.group(ps, ps.ap[:, :256], [(wkv[:, h, kc * 128:(kc + 1) * 128], memb(kc)[1], [wk, memb(kc)[0]]) for kc in range(8)])
                self.act(self.KT[l], self.KT[l].ap[:, h, :], ps, ps.ap[:, :256], AF.Copy)
                st = F.get()
                self.cp(st, st.ap[:, :256], ps, ps.ap[:, :256])
                PS.put(ps)
                fw.dma("sp", self.o_kT[s, l, h], st.ap[:, :256], reads=[st])
                F.put(st)
            W.release(wk)
            self.chk("kv3")
            wv = W.next("wv%d" % l)
            wvv = wv.ap[:, :4096].rearrange("p (k n) -> p k n", k=8)
            for mc in range(2):
                ps = PS.get()
                self.group(ps, ps.ap[:, :512], [(memb(kc)[1][:, mc * 128:(mc + 1) * 128], wvv[:, kc, :], [wv, memb(kc)[0]]) for kc in range(8)])
                self.act(self.VV[l], self.VV[l].ap[:, mc, :], ps, ps.ap[:, :512], AF.Copy)
                st = F.get()
                self.cp(st, st.ap[:, :512], ps, ps.ap[:, :512])
                PS.put(ps)
                fw.dma("sp", self.o_v[s, l, mc * 128:(mc + 1) * 128, :], st.ap[:, :512], reads=[st])
                F.put(st)
            W.release(wv)
        H.put(*mb)

    def layer(self, l, N, last, sidx, is_sample):
        fw, F, H, PS, W = self.fw, self.F, self.H, self.PS, self.W
        X, XB, XAH, UH = self.X, self.XB, self.XAH, self.UH
        HA, HB, HST, KT, VV, OA, OB = self.HA, self.HB, self.HST, self.KT, self.VV, self.OA, self.OB
        ta, tb = min(3, N), min(30, N)

        def wview(w, no, fdim):
            return w.ap[:, :no * fdim].rearrange("p (o f) -> p o f", o=no)

        def xgroup(ps, wv, oi, rhs_bufs, w, nk=8):
            self.group(ps, ps.ap[:, :N], [(wv[:, oi, kc * 128:(kc + 1) * 128], rhs_bufs[kc].ap[:, :N], [w, rhs_bufs[kc]]) for kc in range(nk)])

        fw.op("pool", lambda e: e.tensor_copy(out=self.XAH_t[:, :, 1:4], in_=HA[l].ap), reads=[HA[l]], writes=XAH)
        fw.op("pool", lambda e: e.tensor_copy(out=self.UH_t[:, :, 2:32], in_=HB[l].ap), reads=[HB[l]], writes=UH)
        for g in range(2):
            w = W.next("win%d.%d" % (l, g)); wv = wview(w, 4, 1024)
            for oi in range(4):
                c = 4 * g + oi
                ps = PS.get(); xgroup(ps, wv, oi, XB, w)
                self.act(XAH[c], XAH[c].ap[:, HA_OFF:HA_OFF + N], ps, ps.ap[:, :N], AF.Copy)
                if last:
                    self.act(OA[l], OA[l].ap[:, c, 3 - ta:3], ps, ps.ap[:, N - ta:N], AF.Copy)
                PS.put(ps)
                self.pump(2)
            W.release(w)
        YG = []
        for g in range(2):
            w = W.next("win%d.%d" % (l, 2 + g)); wv = wview(w, 4, 1024)
            for oi in range(4):
                ps = PS.get(); xgroup(ps, wv, oi, XB, w)
                y = F.get(); self.act(y, y.ap[:, :N], ps, ps.ap[:, :N], AF.Gelu_apprx_tanh); YG.append(y)
                PS.put(ps)
                self.pump(2)
            W.release(w)
        w1 = W.next("win%d.4" % l); w2 = W.next("win%d.5" % l)
        wv1, wv2 = wview(w1, 4, 1024), wview(w2, 4, 1024)
        for c in range(4):
            p1 = PS.get(); xgroup(p1, wv1, c, XB, w1)
            p2 = PS.get(); xgroup(p2, wv2, c, XB, w2)
            sg = F.get(); self.act(sg, sg.ap[:, :N], p2, p2.ap[:, :N], AF.Sigmoid)
            PS.put(p2)
            self.tt(UH[c], UH[c].ap[:, HB_OFF:HB_OFF + N], p1, p1.ap[:, :N], sg, sg.ap[:, :N], ALU.mult)
            if last:
                self.tt(OB[l], OB[l].ap[:, c, 30 - tb:30], p1, p1.ap[:, N - tb:N], sg, sg.ap[:, N - tb:N], ALU.mult)
            PS.put(p1); F.put(sg)
            self.pump(2)
        W.release(w1); W.release(w2)
        Q = []
        w = W.next("win%d.6" % l); wv = wview(w, 4, 1024)
        for h in range(4):
            ps = PS.get(); xgroup(ps, wv, h, XB, w)
            q = H.get(); self.cp(q, q.ap[:, :N], ps, ps.ap[:, :N]); Q.append(q)
            PS.put(ps)
        W.release(w)
        fw.op("pool", lambda e: e.tensor_copy(out=HA[l].ap, in_=self.XAH_t[:, :, N + 1:N + 4]), reads=XAH, writes=[HA[l]])
        fw.op("pool", lambda e: e.tensor_copy(out=HB[l].ap, in_=self.UH_t[:, :, N + 2:N + 32]), reads=UH, writes=[HB[l]])
        if last:
            fw.dma("act", self.o_ca[sidx, l], OA[l].ap, reads=[OA[l]])
            fw.dma("act", self.o_cb[sidx, l], OB[l].ap, reads=[OB[l]])

        self.chk("win")
        wd = W.next("da%d" % l); wdv = wd.ap[:, :4096].rearrange("p (a b) -> p a b", a=32)
        wl = W.next("lru%d" % l); wlv = wl.ap[:, :2048].rearrange("p (g n d) -> p g n d", g=2, n=8)
        HG = [None] * 8
        V = [None] * 4
        st = {}

        cb = {}

        def conv_b0(c):
            wdb = W.next("db%d.%d" % (l, c)); wdbv = wdb.ap[:, :31 * 128].rearrange("p (a b) -> p a b", a=31)
            pv = PS.get()
            for k in range(16):
                self.mm(pv, pv.ap[:, :N], wdbv[:, k, :], UH[c].ap[:, 2 + k:2 + k + N], k == 0, False, [wdb, UH[c]])
            cb[c] = (wdb, wdbv, pv)

        def conv_b1(c):
            wdb, wdbv, pv = cb.pop(c)
            for k in range(16, 31):
                self.mm(pv, pv.ap[:, :N], wdbv[:, k, :], UH[c].ap[:, 2 + k:2 + k + N], False, k == 30, [wdb, UH[c]])
            v = F.get(); self.act(v, v.ap[:, :N], pv, pv.ap[:, :N], AF.Identity, bias=self.cc("conv_b_b%d" % l, c), extra=[self.C]); V[c] = v
            PS.put(pv); W.release(wdb)

        def s1(c):
            pc = PS.get()
            self.group(pc, pc.ap[:, :N], [(wdv[:, k * 8 + c, :], XAH[c].ap[:, 1 + k:1 + k + N], [wd, XAH[c]]) for k in range(4)])
            xc = F.get()
            bcol = self.cc("conv_a_b%d" % l, c)
            fw.op("dve", lambda e: e.tensor_scalar(out=xc.ap[:, :N], in0=pc.ap[:, :N], scalar1=bcol, scalar2=None, op0=ALU.add),
                  reads=[pc, self.C], writes=[xc])
            PS.put(pc)
            xcb = H.get(); self.cp(xcb, xcb.ap[:, :N], xc, xc.ap[:, :N], eng="pool")
            st[c] = {"xc": xc, "xcb": xcb}

        def s2_act(c):
            d = st[c]; xcb = d.pop("xcb")
            pr = PS.get(); self.group(pr, pr.ap[:, :N], [(wlv[:, 0, c, :], xcb.ap[:, :N], [wl, xcb])])
            pi = PS.get(); self.group(pi, pi.ap[:, :N], [(wlv[:, 1, c, :], xcb.ap[:, :N], [wl, xcb])])
            H.put(xcb)
            r = F.get(); self.act(r, r.ap[:, :N], pr, pr.ap[:, :N], AF.Sigmoid, bias=self.cc("lru_b_r%d" % l, c), extra=[self.C]); PS.put(pr)
            i = F.get(); self.act(i, i.ap[:, :N], pi, pi.ap[:, :N], AF.Sigmoid, bias=self.cc("lru_b_i%d" % l, c), extra=[self.C]); PS.put(pi)
            a = F.get(); self.act(a, a.ap[:, :N], r, r.ap[:, :N], AF.Exp, scale=self.C8.ap[:, l, c:c + 1], extra=[self.C8])
            d.update(r=r, i=i, a=a)

        def s2_T(c):
            d = st[c]; r, a = d["r"], d["a"]
            self.tt(r, r.ap[:, :N], a, a.ap[:, :N], a, a.ap[:, :N], ALU.mult)

        def s2_S(c):
            r = st[c]["r"]
            self.act(r, r.ap[:, :N], r, r.ap[:, :N], AF.Sqrt, scale=-1.0, bias=1.0)

        def s2_ix(c):
            d = st[c]; i, xc = d["i"], d.pop("xc")
            self.tt(i, i.ap[:, :N], i, i.ap[:, :N], xc, xc.ap[:, :N], ALU.mult, eng="pool"); F.put(xc)

        def s2_dve(c):
            d = st.pop(c); r, i, a = d["r"], d["i"], d["a"]
            self.tt(i, i.ap[:, :N], i, i.ap[:, :N], r, r.ap[:, :N], ALU.mult); F.put(r)
            hh = F.get()
            hst = HST[l]
            fw.op("dve", lambda e: e.tensor_tensor_scan(out=hh.ap[:, :N], data0=a.ap[:, :N], data1=i.ap[:, :N], initial=hst.ap[:, c:c + 1], op0=ALU.mult, op1=ALU.add),
                  reads=[a, i, hst], writes=[hh])
            F.put(a); F.put(i)
            self.cp(hst, hst.ap[:, c:c + 1], hh, hh.ap[:, N - 1:N])
            hg = H.get(); self.tt(hg, hg.ap[:, :N], hh, hh.ap[:, :N], YG[c], YG[c].ap[:, :N], ALU.mult)
            HG[c] = hg
            F.put(hh); F.put(YG[c])
        s1(0)
        s1(1)
        OT = [None] * 4
        kt, vvb = KT[l], VV[l]
        cst = {}

        def c_scores(h):
            pss = [PS.get(), PS.get()]
            pt = [H.get(), H.get()]
            for mc in range(2):
                self.group(pss[mc], pss[mc].ap[:, :N], [(kt.ap[:, h, mc * 128:(mc + 1) * 128], Q[h].ap[:, :N], [kt, Q[h]])])
                self.act(pt[mc], pt[mc].ap[:, :N], pss[mc], pss[mc].ap[:, :N], AF.Exp, scale=SCALE)
                PS.put(pss[mc])
            H.put(Q[h])
            cst[h] = pt

        def c_pv(h):
            pt = cst.pop(h)
            psum_s = PS.get()
            self.group(psum_s, psum_s.ap[:, :N], [(self.ONES.ap, pt[mc].ap[:, :N], [self.ONES, pt[mc]]) for mc in range(2)])
            po = PS.get()
            self.group(po, po.ap[:, :N], [(vvb.ap[:, mc, h * 128:(h + 1) * 128], pt[mc].ap[:, :N], [vvb, pt[mc]]) for mc in range(2)])
            rs = F.get()
            fw.op("dve", lambda e: e.reciprocal(out=rs.ap[:, :N], in_=psum_s.ap[:, :N]), reads=[psum_s], writes=[rs])
            PS.put(psum_s)
            ot = H.get(); OT[h] = ot
            self.tt(ot, ot.ap[:, :N], po, po.ap[:, :N], rs, rs.ap[:, :N], ALU.mult)
            PS.put(po); F.put(rs); H.put(*pt)
        c_scores(0)
        for h in range(4):
            if h + 1 < 4:
                c_scores(h + 1)
            c_pv(h)

        conv_b0(0)
        conv_b1(0)
        for it in range(9):
            if it + 2 < 8:
                s1(it + 2)
            if it < 8:
                s2_act(it)
            if it >= 1:
                s2_S(it - 1)
            if it < 8:
                s2_T(it)
                s2_ix(it)
            if it >= 1:
                s2_dve(it - 1)
            if it < 6:
                if it % 2 == 0:
                    conv_b0(1 + it // 2)
                else:
                    conv_b1(1 + it // 2)
        W.release(wd); W.release(wl)
        if last:
            fw.dma("act", self.o_lru[sidx, l], HST[l].ap, reads=[HST[l]])
        self.chk("brA")

        YB = [None] * 4

        def dst_b(c):
            YB[c] = H.get()
            return YB[c], YB[c].ap[:, :N]
        self.chk("lnB")
        lnb = deque()
        self.layernorm(V, 4, N, self.ONES512, "ln_b_g%d" % l, "ln_b_b%d" % l, AF.Silu, dst_b, defer_to=lnb,
                       on_done=lambda: F.put(*V))

        def pump_lnb(n):
            while n > 0 and lnb:
                lnb.popleft()()
                n -= 1
        pump_lnb(2)

        M = [None] * 8
        MB = [None] * 8

        def gated_proj(gslot0, k, pname, nk, rhs_bufs, combine):
            if nk == 8:
                for g in range(2):
                    wg = W.next("win%d.%d" % (l, gslot0 + g)); wgv = wview(wg, 4, 1024)
                    wp = W.next("%s%d.%d" % (pname, l, 4 * g)); wpv = wview(wp, 4, 1024)
                    for oi in range(4):
                        o = 4 * g + oi
                        pg = PS.get(); xgroup(pg, wgv, oi, XB, wg)
                        po = PS.get(); xgroup(po, wpv, oi, rhs_bufs, wp)
                        G = F.get(); self.act(G, G.ap[:, :N], pg, pg.ap[:, :N], AF.Sigmoid, bias=self.cc("b_gate%d" % l, k * 8 + o), extra=[self.C]); PS.put(pg)
                        combine(o, G, po)
                        PS.put(po); F.put(G)
                        pump_lnb(2)
                    W.release(wg); W.release(wp)
            else:
                wp = W.next("%s%d.0" % (pname, l)); wpv = wview(wp, 8, 512)
                for g in range(2):
                    wg = W.next("win%d.%d" % (l, gslot0 + g)); wgv = wview(wg, 4, 1024)
                    for oi in range(4):
                        o = 4 * g + oi
                        pg = PS.get(); xgroup(pg, wgv, oi, XB, wg)
                        po = PS.get(); xgroup(po, wpv, o, rhs_bufs, wp, nk=4)
                        G = F.get(); self.act(G, G.ap[:, :N], pg, pg.ap[:, :N], AF.Sigmoid, bias=self.cc("b_gate%d" % l, k * 8 + o), extra=[self.C]); PS.put(pg)
                        combine(o, G, po)
                        PS.put(po); F.put(G)
                        pump_lnb(2)
                    W.release(wg)
                W.release(wp)

        def comb_a(o, G, po):
            m = F.get(); M[o] = m
            self.tt(m, m.ap[:, :N], po, po.ap[:, :N], G, G.ap[:, :N], ALU.mult)

        def comb_b(o, G, po):
            t = F.get()
            self.tt(t, t.ap[:, :N], po, po.ap[:, :N], G, G.ap[:, :N], ALU.mult)
            self.tt(M[o], M[o].ap[:, :N], M[o], M[o].ap[:, :N], t, t.ap[:, :N], ALU.add)
            F.put(t)

        def comb_c(o, G, po):
            t = F.get()
            self.tt(t, t.ap[:, :N], po, po.ap[:, :N], G, G.ap[:, :N], ALU.mult)
            mb = H.get(); MB[o] = mb
            self.tt(mb, mb.ap[:, :N], M[o], M[o].ap[:, :N], t, t.ap[:, :N], ALU.add)
            F.put(t); F.put(M[o])
        def comb_first(o, G, po):
            m = F.get(); M[o] = m
            self.tt(m, m.ap[:, :N], po, po.ap[:, :N], G, G.ap[:, :N], ALU.mult)

        def comb_mid(o, G, po):
            t = F.get()
            self.tt(t, t.ap[:, :N], po, po.ap[:, :N], G, G.ap[:, :N], ALU.mult)
            self.tt(M[o], M[o].ap[:, :N], M[o], M[o].ap[:, :N], t, t.ap[:, :N], ALU.add)
            F.put(t)

        def comb_last(o, G, po):
            t = F.get()
            self.tt(t, t.ap[:, :N], po, po.ap[:, :N], G, G.ap[:, :N], ALU.mult)
            mb = H.get(); MB[o] = mb
            self.tt(mb, mb.ap[:, :N], M[o], M[o].ap[:, :N], t, t.ap[:, :N], ALU.add)
            F.put(t); F.put(M[o])
        self.chk("projC")
        gated_proj(11, 2, "pc", 4, OT, comb_first)
        H.put(*OT)
        pump_lnb(100)
        self.chk("projB")
        gated_proj(9, 1, "pb", 4, YB, comb_mid)
        H.put(*YB)
        self.chk("projA")
        gated_proj(7, 0, "pa", 8, HG, comb_last)
        H.put(*HG)
        self.chk("wout")

        R1 = [None] * 8

        def resid(o, po):
            r1 = F.get(); R1[o] = r1
            fw.op("dve", lambda e: e.scalar_tensor_tensor(out=r1.ap[:, :N], in0=X[o].ap[:, :N], scalar=ALPHA, in1=po.ap[:, :N], op0=ALU.mult, op1=ALU.add),
                  reads=[X[o], po], writes=[r1])
        for g in range(2):
            w = W.next("wo%d.%d" % (l, 4 * g)); wv = wview(w, 4, 1024)
            for oi in range(4):
                o = 4 * g + oi
                po = PS.get(); xgroup(po, wv, oi, MB, w)
                resid(o, po)
                PS.put(po)
            W.release(w)
        H.put(*MB)

        def dst_x(c):
            return X[c], X[c].ap[:, :N]

        def post_x(c):
            self.cp(XB[c], XB[c].ap[:, :N], X[c], X[c].ap[:, :N], eng="pool")
        self.chk("ln1_start")
        r1a = list(R1)
        self.layernorm(r1a, 8, N, self.ONES1K, "ln1_g%d" % l, "ln1_b%d" % l, AF.Identity, dst_x, post_x,
                       defer=True, on_done=lambda: F.put(*r1a))

        yield
        self.chk("ln1")
        HF = [None] * NFF
        for s in range(6):
            no = 4 if s < 5 else 2
            wg = W.next("fg%d.%d" % (l, 4 * s)); wgv = wview(wg, no, 1024)
            wu = W.next("fu%d.%d" % (l, 4 * s)); wuv = wview(wu, no, 1024)
            for oi in range(no):
                j = 4 * s + oi
                pg = PS.get(); xgroup(pg, wgv, oi, XB, wg)
                pu = PS.get(); xgroup(pu, wuv, oi, XB, wu)
                sg = F.get(); self.act(sg, sg.ap[:, :N], pg, pg.ap[:, :N], AF.Silu); PS.put(pg)
                hf = H.get(); HF[j] = hf
                self.tt(hf, hf.ap[:, :N], pu, pu.ap[:, :N], sg, sg.ap[:, :N], ALU.mult)
                PS.put(pu); F.put(sg)
                self.pump(2)
            W.release(wg); W.release(wu)
        self.chk("ffn_down")
        for o in range(8):
            w = W.next("fd%d.%d" % (l, o)); wv = w.ap[:, :2816].rearrange("p (o f) -> p o f", o=1)
            po = PS.get()
            self.group(po, po.ap[:, :N], [(wv[:, 0, kc * 128:(kc + 1) * 128], HF[kc].ap[:, :N], [w, HF[kc]]) for kc in range(NFF)])
            resid(o, po)
            PS.put(po)
            self.pump(2)
            W.release(w)
        H.put(*HF)
        self.chk("ln2")
        final = (l == self.depth - 1)
        r1b = list(R1)
        self.layernorm(r1b, 8, N, self.ONES1K, "ln2_g%d" % l, "ln2_b%d" % l, AF.Identity, dst_x, None if final else post_x,
                       defer=True, on_done=lambda: F.put(*r1b))

    def tile_begin(self, src_ap, N):
        fw = self.fw
        X, XB = self.X, self.XB
        fw.dma("sp", self.X_t[:, :, :N], src_ap.rearrange("(c p) n -> p c n", p=128), writes=X)

        def dst_x(c):
            return X[c], X[c].ap[:, :N]

        def post_x(c):
            self.cp(XB[c], XB[c].ap[:, :N], X[c], X[c].ap[:, :N], eng="pool")
        self.layernorm(X, 8, N, self.ONES1K, "ln_in_g", "ln_in_b", AF.Identity, dst_x, post_x)
        self.chk("ln_in")

    def tile_end(self, dst_ap, N):
        self.fw.dma("act", dst_ap.rearrange("(c p) n -> p c n", p=128), self.X_t[:, :, :N], reads=self.X)

    def build(self):
        fw = self.fw
        try:
            self._build()
        except StopBuild:
            pass
        fw.wait_all_dma("pool")
        fw.finish()
        return self.nc

    def _build(self):
        fw = self.fw
        self.phase0()
        self.chk("phase0")
        self.schedule_all()
        TN = self.TN
        done_kv = set()
        for rnd in self.rounds():
            infos = []
            for kind, slot, s, t in rnd:
                self.use_ctx(slot)
                self.use_seq(slot)
                if kind == "p":
                    if s not in done_kv:
                        done_kv.add(s)
                        for l in range(self.depth):
                            fw.op("pool", lambda e, b=self.HA[l]: e.memset(b.ap, 0.0), writes=[self.HA[l]])
                            fw.op("pool", lambda e, b=self.HB[l]: e.memset(b.ap, 0.0), writes=[self.HB[l]])
                            fw.op("pool", lambda e, b=self.HST[l]: e.memset(b.ap, 0.0), writes=[self.HST[l]])
                        self.chk("kv0")
                        self.kv_phase(s)
                        self.chk("kv")
                    N = TN
                    last = (t == self.seq // TN - 1)
                    src = self.xT[s, :, t * TN:(t + 1) * TN]
                    dst = self.yT[s, :, t * TN:(t + 1) * TN]
                else:
                    smp = Buf(None, "smp", dgroup="SMP")
                    for l in range(self.depth):
                        fw.dma("pool", self.HA[l].ap, self.sca[:, l], writes=[self.HA[l]], sem_buf=smp)
                        fw.dma("pool", self.HB[l].ap, self.scb[:, l], writes=[self.HB[l]], sem_buf=smp)
                        fw.dma("pool", self.HST[l].ap, self.slru[:, l], writes=[self.HST[l]], sem_buf=smp)
                        fw.dma("pool", self.KT[l].ap, self.ckT[:, l], writes=[self.KT[l]], sem_buf=smp)
                        fw.dma("pool", self.VV[l].ap, self.cv[:, l], writes=[self.VV[l]], sem_buf=smp)
                        fw.dma("pool", self.OB[l].ap[:, :, 0:14], self.scb[:, l, :, 16:30], writes=[self.OB[l]], sem_buf=smp)
                    N, last = 16, True
                    src = self.xsT
                    dst = self.ysT
                self.tile_begin(src, N)
                infos.append((slot, N, last, s, kind == "s", dst))
            for l in range(self.depth):
                if l >= 1:
                    while self.bg_casts:
                        self.bg_casts.popleft()()
                gens = []
                for slot, N, last, s, is_s, dst in infos:
                    self.use_ctx(slot)
                    self.use_seq(slot)
                    self.chk("xa")
                    self.flush(slot)
                    g = self.layer(l, N, last, s, is_s)
                    next(g)
                    gens.append((slot, g))
                for slot, g in gens:
                    self.use_ctx(slot)
                    self.use_seq(slot)
                    self.flush(slot)
                    for _ in g:
                        pass
            self.chk("tile_out")
            for slot, N, last, s, is_s, dst in infos:
                self.use_ctx(slot)
                self.flush(slot)
                self.tile_end(dst, N)


def assemble(results, nseq, seq, with_sample=True):
    n = len(results)
    L = DEPTH
    y = np.concatenate([r["yT"].transpose(0, 2, 1) for r in results], axis=0)
    ys = np.stack([r["ysT"].T for r in results], axis=0)

    def ca(idx):
        a = np.concatenate([r["o_ca"][idx] for r in results], axis=0)
        return np.ascontiguousarray(a.transpose(1, 0, 4, 3, 2)).reshape(L, a.shape[0], 3, 1024)

    def lru(idx):
        a = np.concatenate([r["o_lru"][idx] for r in results], axis=0)
        return np.ascontiguousarray(a.transpose(1, 0, 3, 2)).reshape(L, a.shape[0], 1024)

    def cb(idx):
        a = np.concatenate([r["o_cb"][idx] for r in results], axis=0)
        return np.ascontiguousarray(a.transpose(1, 0, 4, 3, 2)).reshape(L, a.shape[0], 30, 512)
    pi = slice(0, nseq)
    si = slice(nseq, nseq + 1)
    kT = np.concatenate([r["o_kT"] for r in results], axis=0)
    mk = np.ascontiguousarray(kT.transpose(1, 0, 4, 2, 3))
    v = np.concatenate([r["o_v"] for r in results], axis=0)
    mv = np.ascontiguousarray(v.transpose(1, 0, 2, 3)).reshape(L, v.shape[0], N_MEM, 4, 128)
    return (y, ys, ca(pi), lru(pi), cb(pi), mk, mv, ca(si), lru(si), cb(si))


_CACHE = {}


def kernel(**inputs):
    n = 8
    nseq, seq = 2, 4096
    inp = {k: np.asarray(v) for k, v in inputs.items()}
    shared = prep_shared(inp)
    in_maps = []
    for i in range(n):
        m = prep_core(inp, i, nseq, seq)
        m.update(shared)
        in_maps.append(m)
    nc = Builder(nseq, seq).build()
    res = run_bass_kernel_spmd(nc, in_maps, core_ids=list(range(n)))
    return assemble(res.results, nseq, seq)
```

```python
import numpy as np
from collections import deque
import concourse.bass as bass
import concourse.mybir as mybir
from concourse.bass_utils import run_bass_kernel_spmd

F32 = mybir.dt.float32
BF16 = mybir.dt.bfloat16
AF = mybir.ActivationFunctionType
ALU = mybir.AluOpType

D = 1024
DEPTH = 4
D_IN = 6656
D_FF = 2816
NFF = 22
N_MEM = 256
ALPHA = float((2 * DEPTH) ** 0.25)
LN_EPS = 1e-5
SCALE = float(128 ** -0.5)
HA_OFF = 4
HB_OFF = 32

ENGS = ("pe", "act", "dve", "pool", "sp")


class Buf:
    __slots__ = ("ap", "w", "r", "name", "dsem", "psum", "dgroup")

    def __init__(self, ap, name="", psum=False, dgroup=None):
        self.ap = ap
        self.psum = psum
        self.dgroup = dgroup
        self.w = None
        self.r = {}
        self.name = name
        self.dsem = None


class FW:
    def __init__(self, nc):
        self.nc = nc
        self.prog = {e: [] for e in ENGS}
        self.count = {e: 0 for e in ENGS}
        self.seen = {e: {} for e in ENGS}
        self.sems = {}
        self.dcount = {}
        self.n_dsem = 0
        self.pool_inflight = deque()
        self.shared = set()
        self.groups = {}
        self.phase = ""
        self.labels = {e: [] for e in ENGS}
        for e in ENGS:
            self.sems[e] = nc.alloc_semaphore("sem_" + e)

    def _deps(self, eng, reads, writes):
        deps = {}

        def add(k, v):
            if deps.get(k, 0) < v:
                deps[k] = v
        for b in reads:
            if b.w is not None:
                add(*b.w)
            if b.psum:
                for k, v in b.r.items():
                    if k != eng:
                        add(k, v)
        for b in writes:
            if b.w is not None and b.w[0] != eng:
                add(*b.w)
            for k, v in b.r.items():
                if k != eng:
                    add(k, v)
        waits = []
        seen = self.seen[eng]
        for k, v in deps.items():
            if seen.get(k, 0) < v:
                seen[k] = v
                waits.append((k, v))
        return waits

    def op(self, eng, fn, reads=(), writes=(), signal=True):
        waits = self._deps(eng, reads, writes)
        if signal:
            self.count[eng] += 1
            tk = (eng, self.count[eng])
        else:
            tk = (eng, self.count[eng] + 1)
        for b in writes:
            b.w = tk
            b.r = {}
        for b in reads:
            if b.r.get(eng, 0) < tk[1]:
                b.r[eng] = tk[1]
        self.prog[eng].append((waits, fn, eng if signal else None, 1))
        self.labels[eng].append(self.phase)
        return tk

    def dsem_for(self, b):
        if b.dsem is None and b.dgroup is not None:
            if b.dgroup not in self.groups:
                k = "g_" + b.dgroup
                self.n_dsem += 1
                self.sems[k] = self.nc.alloc_semaphore("dsem_" + b.dgroup)
                self.dcount[k] = 0
                self.groups[b.dgroup] = k
                self.shared.add(k)
            b.dsem = self.groups[b.dgroup]
        if b.dsem is None:
            k = "d%d" % self.n_dsem
            self.n_dsem += 1
            self.sems[k] = self.nc.alloc_semaphore("dsem_%d" % self.n_dsem)
            self.dcount[k] = 0
            b.dsem = k
        return b.dsem

    def dma(self, q, out, in_, reads=(), writes=(), sem_buf=None, max_inflight=6):
        if sem_buf is None:
            sem_buf = (list(writes) + list(reads))[0]
        k = self.dsem_for(sem_buf)
        waits = self._deps(q, reads, writes)
        if k in self.shared and self.dcount[k] > 0 and self.seen[q].get(k, 0) < self.dcount[k]:
            self.seen[q][k] = self.dcount[k]
            waits.append((k, self.dcount[k]))
        if q == "pool":
            while len(self.pool_inflight) >= max_inflight:
                ok, ov = self.pool_inflight.popleft()
                if self.seen[q].get(ok, 0) < ov:
                    self.seen[q][ok] = ov
                    waits.append((ok, ov))
        self.dcount[k] += 16
        tk = (k, self.dcount[k])
        for b in writes:
            b.w = tk
            b.r = {}
        for b in reads:
            b.r[k] = tk[1]
        if q == "pool":
            self.pool_inflight.append(tk)
        self.prog[q].append((waits, lambda e: e.dma_start(out=out, in_=in_), k, 16))
        return tk

    def wait_all_dma(self, q):
        waits = [(k, v) for k, v in self.dcount.items() if v > 0]
        self.prog[q].append((waits, None, None, 0))

    def finish(self):
        nc = self.nc
        with nc.Block() as block:
            def mk(eng):
                def body(e):
                    for waits, fn, sk, inc in self.prog[eng]:
                        for k, v in waits:
                            e.wait_ge(self.sems[k], v)
                        if fn is not None:
                            ins = fn(e)
                            if sk is not None:
                                ins.then_inc(self.sems[sk], inc)
                return body
            block.tensor(mk("pe"))
            block.scalar(mk("act"))
            block.vector(mk("dve"))
            block.gpsimd(mk("pool"))
            block.sync(mk("sp"))


class Pool:
    def __init__(self, bufs, name):
        self.free = deque(bufs)
        self.name = name
        self.low = len(bufs)

    def get(self):
        if not self.free:
            raise RuntimeError("pool %s exhausted" % self.name)
        b = self.free.popleft()
        self.low = min(self.low, len(self.free))
        return b

    def put(self, *bs):
        for b in bs:
            self.free.append(b)


class WStream:
    def __init__(self, fw, slots, q="sp"):
        self.fw = fw
        self.free = deque(slots)
        self.pending = deque()
        self.loaded = deque()
        self.q = q

    def schedule(self, name, src_ap, per_part, src_buf):
        self.pending.append((name, src_ap, per_part, src_buf))

    def _pump(self):
        while self.free and self.pending:
            name, src_ap, per_part, src_buf = self.pending.popleft()
            slot = self.free.popleft()
            dst = slot.ap[:, :per_part]
            if len(src_ap.shape) == 3:
                dst = dst.rearrange("p (o f) -> p o f", o=src_ap.shape[1])
            self.fw.dma(self.q, dst, src_ap, reads=[src_buf], writes=[slot], sem_buf=slot)
            self.loaded.append((name, slot))

    def next(self, name):
        self._pump()
        nm, slot = self.loaded.popleft()
        assert nm == name, (nm, name)
        return slot

    def release(self, slot):
        self.free.append(slot)
        self._pump()


def const_layout():
    off = {}
    n = 0

    def add(name, w):
        nonlocal n
        off[name] = n
        n += w
    add("ln_in_g", 8)
    add("ln_in_b", 8)
    for l in range(DEPTH):
        add("b_gate%d" % l, 24)
        add("conv_a_b%d" % l, 8)
        add("lru_b_r%d" % l, 8)
        add("lru_b_i%d" % l, 8)
        add("lru_lambda%d" % l, 8)
        add("conv_b_b%d" % l, 4)
        add("ln_b_g%d" % l, 4)
        add("ln_b_b%d" % l, 4)
        add("ln1_g%d" % l, 8)
        add("ln1_b%d" % l, 8)
        add("ln2_g%d" % l, 8)
        add("ln2_b%d" % l, 8)
        add("conv_a_w%d" % l, 32)
        add("conv_b_w%d" % l, 124)
    return off, n


COFF, NCONST = const_layout()

WSHAPES = {
    "w_in_t": (DEPTH * 52 * 128, 1024),
    "proj_a_t": (DEPTH * 8 * 128, 1024),
    "proj_b_t": (DEPTH * 8 * 128, 512),
    "proj_c_t": (DEPTH * 8 * 128, 512),
    "w_out_t": (DEPTH * 8 * 128, 1024),
    "lru_w": (DEPTH * 128, 2048),
    "wk_t": (DEPTH * 4 * 128, 1024),
    "wv_m": (DEPTH * 128, 4096),
    "wg_t": (DEPTH * NFF * 128, 1024),
    "wu_t": (DEPTH * NFF * 128, 1024),
    "wd_t": (DEPTH * 8 * 128, 2816),
}


def _tile_w(w):
    L, K, Nout = w.shape
    t = w.reshape(L, K // 128, 128, Nout // 128, 128)
    t = t.transpose(0, 3, 2, 1, 4)
    return np.ascontiguousarray(t).reshape(L * (Nout // 128) * 128, K)


def _pcol(v, nch):
    v = np.asarray(v)
    lead = v.shape[:-1]
    t = v.reshape(lead + (nch, 128))
    t = np.moveaxis(t, -1, 0)
    return np.ascontiguousarray(t)


def prep_shared(inp):
    sh = {}
    sh["w_in_t"] = _tile_w(inp["w_in"])
    sh["proj_a_t"] = _tile_w(inp["proj_a"])
    sh["proj_b_t"] = _tile_w(inp["proj_b"])
    sh["proj_c_t"] = _tile_w(inp["proj_c"])
    sh["w_out_t"] = _tile_w(inp["w_out"])
    sh["wk_t"] = _tile_w(inp["w_mem_k"])
    sh["wg_t"] = _tile_w(inp["w_ffn_gate"])
    sh["wu_t"] = _tile_w(inp["w_ffn_up"])
    sh["wd_t"] = _tile_w(inp["w_ffn_down"])
    wr = np.asarray(inp["lru_w_r"]).transpose(0, 2, 1, 3)
    wi = np.asarray(inp["lru_w_i"]).transpose(0, 2, 1, 3)
    sh["lru_w"] = np.ascontiguousarray(np.stack([wr, wi], axis=2)).reshape(DEPTH * 128, 2048)
    wv = np.asarray(inp["w_mem_v"]).reshape(DEPTH, 8, 128, 512).transpose(0, 2, 1, 3)
    sh["wv_m"] = np.ascontiguousarray(wv).reshape(DEPTH * 128, 4096)
    c = np.zeros((128, NCONST), np.float32)

    def put(name, arr):
        arr = np.asarray(arr, np.float32).reshape(128, -1)
        c[:, COFF[name]:COFF[name] + arr.shape[1]] = arr
    put("ln_in_g", _pcol(inp["ln_in_g"], 8))
    put("ln_in_b", _pcol(inp["ln_in_b"], 8))
    for l in range(DEPTH):
        put("b_gate%d" % l, _pcol(inp["b_gate"][l], 8))
        put("conv_a_b%d" % l, _pcol(inp["conv_a_b"][l], 8))
        put("lru_b_r%d" % l, _pcol(np.asarray(inp["lru_b_r"][l]).reshape(-1), 8))
        put("lru_b_i%d" % l, _pcol(np.asarray(inp["lru_b_i"][l]).reshape(-1), 8))
        put("lru_lambda%d" % l, _pcol(inp["lru_lambda"][l], 8))
        put("conv_b_b%d" % l, _pcol(inp["conv_b_b"][l], 4))
        put("ln_b_g%d" % l, _pcol(inp["ln_b_g"][l], 4))
        put("ln_b_b%d" % l, _pcol(inp["ln_b_b"][l], 4))
        put("ln1_g%d" % l, _pcol(inp["ln1_g"][l], 8))
        put("ln1_b%d" % l, _pcol(inp["ln1_b"][l], 8))
        put("ln2_g%d" % l, _pcol(inp["ln2_g"][l], 8))
        put("ln2_b%d" % l, _pcol(inp["ln2_b"][l], 8))
        put("conv_a_w%d" % l, _pcol(inp["conv_a_w"][l], 8))
        cb = _pcol(inp["conv_b_w"][l], 4)
        put("conv_b_w%d" % l, np.ascontiguousarray(cb.transpose(0, 2, 1)))
    sh["consts"] = c
    return sh


def prep_core(inp, i, nseq, seq):
    m = {}
    xp = np.asarray(inp["x_prompt"])
    m["xT"] = np.ascontiguousarray(xp[nseq * i:nseq * (i + 1), :seq].transpose(0, 2, 1))
    m["xsT"] = np.ascontiguousarray(np.asarray(inp["x_sample"])[i].T)
    m["memT"] = np.ascontiguousarray(np.asarray(inp["mem_prompt"])[nseq * i:nseq * (i + 1)].transpose(0, 2, 1))
    sca = np.asarray(inp["state_conv_a"])[:, i]
    m["sca"] = np.ascontiguousarray(_pcol(sca, 8).transpose(0, 1, 3, 2))
    m["slru"] = _pcol(np.asarray(inp["state_lru"])[:, i], 8)
    scb = np.asarray(inp["state_conv_b"])[:, i]
    m["scb"] = np.ascontiguousarray(_pcol(scb, 4).transpose(0, 1, 3, 2))
    ck = np.asarray(inp["cache_mem_k"])[:, i]
    m["ckT"] = np.ascontiguousarray(ck.transpose(3, 0, 2, 1))
    cv = np.asarray(inp["cache_mem_v"])[:, i].reshape(DEPTH, 2, 128, 512)
    m["cv"] = np.ascontiguousarray(cv.transpose(2, 0, 1, 3))
    return m


class StopBuild(Exception):
    pass


class Builder:
    stop_at = None

    def chk(self, name):
        self.fw.phase = name
        if self.stop_at == name:
            raise StopBuild(name)

    def use_ctx(self, i):
        self.X, self.XB, self.X_t = self.ctx_bufs[i]
        self.cur_slot = i

    def pump(self, n=2):
        for slot, dq in self.deferred.items():
            if slot == self.cur_slot:
                continue
            while n > 0 and dq:
                dq.popleft()()
                n -= 1

    def flush(self, slot):
        dq = self.deferred[slot]
        while dq:
            dq.popleft()()

    def use_seq(self, i):
        for k, v in self.seq_bufs[i].items():
            setattr(self, k, v)

    def __init__(self, nseq, seq, depth=DEPTH, tile=512, with_sample=True, n_wslots=4, NF=23, NH=24):
        self.nseq, self.seq, self.depth, self.TN, self.with_sample = nseq, seq, depth, tile, with_sample
        nc = self.nc = bass.Bass("TRN2", target_bir_lowering=False)
        fw = self.fw = FW(nc)
        L = DEPTH
        di = lambda name, shape: nc.dram_tensor(name, list(shape), F32, kind="ExternalInput").ap()
        do = lambda name, shape: nc.dram_tensor(name, list(shape), F32, kind="ExternalOutput").ap()
        self.xT = di("xT", [nseq, D, seq])
        self.xsT = di("xsT", [D, 16])
        self.memT = di("memT", [nseq, D, N_MEM])
        self.sca = di("sca", [128, L, 8, 3])
        self.slru = di("slru", [128, L, 8])
        self.scb = di("scb", [128, L, 4, 30])
        self.ckT = di("ckT", [128, L, 4, N_MEM])
        self.cv = di("cv", [128, L, 2, 512])
        self.consts_d = di("consts", [128, NCONST])
        self.win = {k: di(k, v) for k, v in WSHAPES.items()}
        self.wsc = {k: nc.dram_tensor("sc_" + k, list(v), BF16).ap() for k, v in WSHAPES.items()}
        self.wsc_buf = {(k, l): Buf(None, "sc_%s%d" % (k, l), dgroup="SC%d" % l) for k in WSHAPES for l in range(L)}
        self.da_sc = nc.dram_tensor("sc_da", [L * 128, 4096], BF16).ap()
        self.db_sc = nc.dram_tensor("sc_db", [L * 4 * 128, 31 * 128], BF16).ap()
        self.da_buf = [Buf(None, "sc_da%d" % l, dgroup="SC%d" % l) for l in range(L)]
        self.db_buf = [Buf(None, "sc_db%d" % l, dgroup="SC%d" % l) for l in range(L)]
        self.yT = do("yT", [nseq, D, seq])
        self.ysT = do("ysT", [D, 16])
        self.o_ca = do("o_ca", [nseq + 1, L, 128, 8, 3])
        self.o_lru = do("o_lru", [nseq + 1, L, 128, 8])
        self.o_cb = do("o_cb", [nseq + 1, L, 128, 4, 30])
        self.o_kT = do("o_kT", [nseq, L, 4, 128, N_MEM])
        self.o_v = do("o_v", [nseq, L, N_MEM, 512])

        sb = lambda name, shape, dt: nc.alloc_sbuf_tensor(name, list(shape), dt)
        TN = tile
        self.ctx_bufs = []
        for i in range(2):
            xt = sb("X%d" % i, [128, 8, TN], F32)
            xbt = sb("XB%d" % i, [128, 8, TN], BF16)
            self.ctx_bufs.append(([Buf(xt[:, c, :], "X%d_%d" % (i, c), dgroup="X%d" % i) for c in range(8)],
                                  [Buf(xbt[:, c, :], "XB%d_%d" % (i, c)) for c in range(8)], xt))
        self.XAH_t = sb("XAH", [128, 8, HA_OFF + TN], BF16)
        self.XAH = [Buf(self.XAH_t[:, c, :], "XAH%d" % c) for c in range(8)]
        self.UH_t = sb("UH", [128, 4, HB_OFF + TN], BF16)
        self.UH = [Buf(self.UH_t[:, c, :], "UH%d" % c) for c in range(4)]
        self.seq_bufs = []
        for i in range(2):
            kt = sb("KT%d" % i, [128, L, 4, N_MEM], BF16)
            vv = sb("VV%d" % i, [128, L, 2, 512], BF16)
            ha = sb("HA%d" % i, [128, L, 8, 3], BF16)
            hb = sb("HB%d" % i, [128, L, 4, 30], BF16)
            hs = sb("HST%d" % i, [128, L, 8], F32)
            oa = sb("OA%d" % i, [128, L, 8, 3], F32)
            ob = sb("OB%d" % i, [128, L, 4, 30], F32)
            self.seq_bufs.append(dict(
                KT=[Buf(kt[:, l], "KT%d_%d" % (i, l), dgroup="S%d" % i) for l in range(L)],
                VV=[Buf(vv[:, l], "VV%d_%d" % (i, l), dgroup="S%d" % i) for l in range(L)],
                HA=[Buf(ha[:, l], "HA%d_%d" % (i, l), dgroup="S%d" % i) for l in range(L)],
                HB=[Buf(hb[:, l], "HB%d_%d" % (i, l), dgroup="S%d" % i) for l in range(L)],
                HST=[Buf(hs[:, l], "HST%d_%d" % (i, l), dgroup="S%d" % i) for l in range(L)],
                OA=[Buf(oa[:, l], "OA%d_%d" % (i, l), dgroup="S%d" % i) for l in range(L)],
                OB=[Buf(ob[:, l], "OB%d_%d" % (i, l), dgroup="S%d" % i) for l in range(L)]))
        ft = sb("FP", [128, NF, TN], F32)
        self.F = Pool([Buf(ft[:, i, :], "F%d" % i, dgroup="F") for i in range(NF)], "F")
        ht = sb("HP", [128, NH, TN], BF16)
        self.H = Pool([Buf(ht[:, i, :], "H%d" % i) for i in range(NH)], "H")
        wt = sb("WS", [128, n_wslots, 4096], BF16)
        wslots = [Buf(wt[:, i, :], "W%d" % i) for i in range(n_wslots)]
        self.W = WStream(fw, wslots, q="sp")
        self.DGS = wslots[:2]
        self.C = Buf(sb("C", [128, NCONST], F32)[:], "C")
        self.C8 = Buf(sb("C8", [128, L, 8], F32)[:], "C8")
        self.C16 = Buf(sb("C16", [128, L, 8], F32)[:], "C16")
        self.CH = self.C
        self.IDENT = Buf(sb("IDENT", [128, 128], F32)[:], "IDENT")
        self.ONES = Buf(sb("ONES", [128, 128], BF16)[:], "ONES")
        self.ONES1K = Buf(sb("ONES1K", [128, 128], BF16)[:], "ONES1K")
        self.ONES512 = Buf(sb("ONES512", [128, 128], BF16)[:], "ONES512")
        self.deferred = {0: deque(), 1: deque()}
        self.use_ctx(0)
        self.use_seq(0)
        self.PS = Pool([Buf(nc.alloc_psum_tensor("ps%d" % i, [128, 512], F32)[:], "ps%d" % i, psum=True) for i in range(8)], "PS")
        self.sbuf_left = nc.sbuf_bytes_remaining

    def cc(self, name, j=0, w=1):
        o = COFF[name] + j
        return self.C.ap[:, o:o + w]

    def ch(self, name, j=0, w=1):
        o = COFF[name] + j
        return self.CH.ap[:, o:o + w]

    def mm(self, ps, out_ap, lhsT, rhs, start, stop, reads):
        self.fw.op("pe", lambda e: e.matmul(out_ap, lhsT=lhsT, rhs=rhs, start=start, stop=stop),
                   reads=reads, writes=[ps], signal=stop)

    def group(self, ps, out_ap, items):
        n = len(items)
        for i, (lt, rh, rd) in enumerate(items):
            self.mm(ps, out_ap, lt, rh, i == 0, i == n - 1, rd)

    def act(self, out_b, out_ap, in_b, in_ap, func, scale=None, bias=None, extra=()):
        kw = {}
        rd = [in_b] + list(extra)
        if scale is not None:
            kw["scale"] = scale
        if bias is not None:
            kw["bias"] = bias
        self.fw.op("act", lambda e: e.activation(out=out_ap, in_=in_ap, func=func, **kw), reads=rd, writes=[out_b])

    def tt(self, out_b, out_ap, a_b, a_ap, b_b, b_ap, op, eng="dve"):
        self.fw.op(eng, lambda e: e.tensor_tensor(out=out_ap, in0=a_ap, in1=b_ap, op=op), reads=[a_b, b_b], writes=[out_b])

    def bg_step(self):
        self.bg_tick += 1
        if self.bg_casts and self.bg_tick % 3 == 0:
            self.bg_casts.popleft()()

    def cp(self, out_b, out_ap, in_b, in_ap, eng="dve", extra_w=(), extra_r=()):
        self.fw.op(eng, lambda e: e.tensor_copy(out=out_ap, in_=in_ap), reads=[in_b] + list(extra_r),
                   writes=[out_b] + list(extra_w))
        if eng == "pool":
            self.bg_step()

    def phase0(self):
        fw, nc = self.fw, self.nc
        fw.dma("pool", self.C.ap, self.consts_d, writes=[self.C])
        self.bg_casts = deque()
        self.bg_tick = 0
        order = [("wk_t", l) for l in range(DEPTH)] + [("wv_m", l) for l in range(DEPTH)]
        order += [(k, l) for l in range(DEPTH) for k in WSHAPES if k not in ("wk_t", "wv_m")]
        for k, l in order:
            rows, Fdim = WSHAPES[k]
            per_l = rows // DEPTH
            src = self.win[k][l * per_l:(l + 1) * per_l, :]
            dst = self.wsc[k][l * per_l:(l + 1) * per_l, :]
            if Fdim > 2048:
                src = src.rearrange("r (a b) -> r a b", b=Fdim // 2)
                dst = dst.rearrange("r (a b) -> r a b", b=Fdim // 2)
            elif Fdim == 1024:
                src = src.rearrange("(r two) f -> r (two f)", two=2)
                dst = dst.rearrange("(r two) f -> r (two f)", two=2)
            th = (lambda dst=dst, src=src, k=k, l=l: fw.dma("pool", dst, src, writes=[self.wsc_buf[(k, l)]], max_inflight=1))
            if l == 0 or k in ("wk_t", "wv_m"):
                th()
            else:
                self.bg_casts.append(th)
        fw.op("dve", lambda e: e.memset(self.ONES.ap, 1.0), writes=[self.ONES])
        fw.op("dve", lambda e: e.memset(self.ONES1K.ap, 1.0 / 1024), writes=[self.ONES1K])
        fw.op("dve", lambda e: e.memset(self.ONES512.ap, 1.0 / 512), writes=[self.ONES512])
        fw.op("dve", lambda e: e.memset(self.IDENT.ap, 1.0), writes=[self.IDENT])
        fw.op("pool", lambda e: e.affine_select(out=self.IDENT.ap, in_=self.IDENT.ap, pattern=[[1, 128]],
                                                 compare_op=ALU.is_equal, fill=0.0, base=0, channel_multiplier=-1),
              reads=[self.IDENT], writes=[self.IDENT])
        for l in range(DEPTH):
            for nm, w in (("b_gate%d" % l, 24), ("lru_b_r%d" % l, 8), ("lru_b_i%d" % l, 8)):
                col = self.cc(nm, 0, w)
                fw.op("dve", lambda e, col=col: e.tensor_scalar(out=col, in0=col, scalar1=0.5, scalar2=None, op0=ALU.mult),
                      reads=[self.C], writes=[self.C])
        for l in range(DEPTH):
            lam = self.cc("lru_lambda%d" % l, 0, 8)
            c8 = self.C8.ap[:, l, :]
            self.act(self.C8, c8, self.C, lam, AF.Sigmoid)
            self.act(self.C8, c8, self.C8, c8, AF.Ln)
            self.fw.op("dve", lambda e, c8=c8: e.tensor_scalar(out=c8, in0=c8, scalar1=8.0, scalar2=None, op0=ALU.mult),
                       reads=[self.C8], writes=[self.C8])
            c16 = self.C16.ap[:, l, :]
            self.fw.op("dve", lambda e, c8=c8, c16=c16: e.tensor_scalar(out=c16, in0=c8, scalar1=0.5, scalar2=None, op0=ALU.mult),
                       reads=[self.C8], writes=[self.C16])
        n = 0
        for l in range(self.depth):
            st = self.DGS[n % 2]; n += 1
            stv = st.ap.rearrange("p (a b) -> p a b", a=32)
            for k in range(4):
                for c in range(8):
                    col = self.cc("conv_a_w%d" % l, k * 8 + c)
                    oap = stv[:, k * 8 + c, :]
                    fw.op("dve", lambda e, oap=oap, col=col: e.tensor_scalar(out=oap, in0=self.IDENT.ap, scalar1=col, scalar2=None, op0=ALU.mult),
                          reads=[self.IDENT, self.C], writes=[st])
            fw.dma("pool", self.da_sc[l * 128:(l + 1) * 128, :], st.ap,
                   reads=[st], writes=[self.da_buf[l]], sem_buf=self.da_buf[l])
            for c in range(4):
                st = self.DGS[n % 2]; n += 1
                stv = st.ap.rearrange("p (a b) -> p a b", a=32)
                for k in range(31):
                    col = self.cc("conv_b_w%d" % l, c * 31 + k)
                    oap = stv[:, k, :]
                    fw.op("dve", lambda e, oap=oap, col=col: e.tensor_scalar(out=oap, in0=self.IDENT.ap, scalar1=col, scalar2=None, op0=ALU.mult),
                          reads=[self.IDENT, self.C], writes=[st])
                r0 = (l * 4 + c) * 128
                fw.dma("pool", self.db_sc[r0:r0 + 128, :], st.ap[:, 0:31 * 128],
                       reads=[st], writes=[self.db_buf[l]], sem_buf=self.db_buf[l])

    def _blk(self, key, l, nper, o0, no):
        r0 = (l * nper + o0) * 128
        Fdim = WSHAPES[key][1]
        ap = self.wsc[key][r0:r0 + no * 128, :].rearrange("(o p) f -> p o f", p=128)
        return ap, no * Fdim

    def sched_kv(self, l):
        W = self.W
        ap, pp = self._blk("wk_t", l, 4, 0, 4)
        W.schedule("wk%d" % l, ap, pp, self.wsc_buf[("wk_t", l)])
        W.schedule("wv%d" % l, self.wsc["wv_m"][l * 128:(l + 1) * 128, :], 4096, self.wsc_buf[("wv_m", l)])

    def sched_layer(self, l, part):
        W = self.W

        def win(s):
            ap, pp = self._blk("w_in_t", l, 52, 4 * s, 4)
            W.schedule("win%d.%d" % (l, s), ap, pp, self.wsc_buf[("w_in_t", l)])

        def tiled(key, nm, nper, o0, no):
            ap, pp = self._blk(key, l, nper, o0, no)
            W.schedule("%s%d.%d" % (nm, l, o0), ap, pp, self.wsc_buf[(key, l)])
        if part == 2:
            for s in range(6):
                no = 4 if s < 5 else 2
                tiled("wg_t", "fg", NFF, 4 * s, no)
                tiled("wu_t", "fu", NFF, 4 * s, no)
            for o in range(8):
                tiled("wd_t", "fd", 8, o, 1)
            return
        for s in range(7):
            win(s)
        W.schedule("da%d" % l, self.da_sc[l * 128:(l + 1) * 128, :], 4096, self.da_buf[l])
        W.schedule("lru%d" % l, self.wsc["lru_w"][l * 128:(l + 1) * 128, :], 2048, self.wsc_buf[("lru_w", l)])
        for c in range(4):
            r0 = (l * 4 + c) * 128
            W.schedule("db%d.%d" % (l, c), self.db_sc[r0:r0 + 128, :], 31 * 128, self.db_buf[l])
        tiled("proj_c_t", "pc", 8, 0, 8)
        for g in range(2):
            win(11 + g)
        tiled("proj_b_t", "pb", 8, 0, 8)
        for g in range(2):
            win(9 + g)
        for g in range(2):
            win(7 + g)
            tiled("proj_a_t", "pa", 8, 4 * g, 4)
        for g in range(2):
            tiled("w_out_t", "wo", 8, 4 * g, 4)

    def rounds(self):
        nt = self.seq // self.TN
        rs = []
        for s0 in range(0, self.nseq, 2):
            seqs = list(range(s0, min(s0 + 2, self.nseq)))
            for t in range(nt):
                rs.append([("p", i, s, t) for i, s in enumerate(seqs)])
        if self.with_sample:
            rs.append([("s", 0, self.nseq, 0)])
        return rs

    def schedule_all(self):
        done_kv = set()
        for rnd in self.rounds():
            for kind, slot, s, t in rnd:
                if kind == "p" and s not in done_kv:
                    done_kv.add(s)
                    for l in range(self.depth):
                        self.sched_kv(l)
            for l in range(self.depth):
                for _ in rnd:
                    self.sched_layer(l, 1)
                for _ in rnd:
                    self.sched_layer(l, 2)

    def layernorm(self, src, nch, N, ones_b, gname, bname, func, dst, post=None, defer=False, on_done=None, defer_to=None, eps=LN_EPS):
        F, H, PS = self.F, self.H, self.PS
        rb, rsq = [], []
        for c in range(nch):
            q = H.get(); self.act(q, q.ap[:, :N], src[c], src[c].ap[:, :N], AF.Square); rsq.append(q)
            b = H.get(); self.cp(b, b.ap[:, :N], src[c], src[c].ap[:, :N]); rb.append(b)
        pm = PS.get(); pq = PS.get()
        self.group(pm, pm.ap[:, :N], [(ones_b.ap, rb[c].ap[:, :N], [ones_b, rb[c]]) for c in range(nch)])
        self.group(pq, pq.ap[:, :N], [(ones_b.ap, rsq[c].ap[:, :N], [ones_b, rsq[c]]) for c in range(nch)])
        H.put(*rb); H.put(*rsq)
        t = F.get()
        th = []
        th.append(lambda: self.act(t, t.ap[:, :N], pm, pm.ap[:, :N], AF.Square))

        def _var():
            self.tt(t, t.ap[:, :N], pq, pq.ap[:, :N], t, t.ap[:, :N], ALU.subtract)
            PS.put(pq)
        th.append(_var)
        th.append(lambda: self.fw.op("dve", lambda e: e.tensor_scalar(out=t.ap[:, :N], in0=t.ap[:, :N], scalar1=0.0, scalar2=eps, op0=ALU.max, op1=ALU.add),
                                     reads=[t], writes=[t]))
        th.append(lambda: self.act(t, t.ap[:, :N], t, t.ap[:, :N], AF.Sqrt))
        th.append(lambda: self.fw.op("dve", lambda e: e.reciprocal(out=t.ap[:, :N], in_=t.ap[:, :N]), reads=[t], writes=[t]))
        for c in range(nch):
            def _n1(c=c):
                s = src[c]
                self.tt(s, s.ap[:, :N], s, s.ap[:, :N], pm, pm.ap[:, :N], ALU.subtract)
                self.tt(s, s.ap[:, :N], s, s.ap[:, :N], t, t.ap[:, :N], ALU.mult)

            def _n2(c=c):
                s = src[c]
                db, dap = dst(c)
                self.act(db, dap, s, s.ap[:, :N], func, scale=self.cc(gname, c), bias=self.cc(bname, c), extra=[self.C])
                if post is not None:
                    post(c)
            th.append(_n1)
            th.append(_n2)

        def _fin():
            PS.put(pm)
            F.put(t)
            if on_done is not None:
                on_done()
        th.append(_fin)
        if defer_to is not None:
            defer_to.extend(th)
        elif defer:
            self.deferred[self.cur_slot].extend(th)
        else:
            for f in th:
                f()

    def kv_phase(self, s):
        fw, F, H, PS, W = self.fw, self.F, self.H, self.PS, self.W
        mf = [F.get() for _ in range(4)]
        mb = [H.get() for _ in range(4)]
        for i in range(4):
            src = self.memT[s, i * 256:(i + 1) * 256, :].rearrange("(c p) m -> p c m", p=128)
            fw.dma("sp", mf[i].ap[:, :512].rearrange("p (c m) -> p c m", c=2), src, writes=[mf[i]])
            self.cp(mb[i], mb[i].ap[:, :512], mf[i], mf[i].ap[:, :512])
        F.put(*mf)
        self.chk("kv1")

        def memb(kc):
            return mb[kc // 2], mb[kc // 2].ap[:, (kc % 2) * 256:(kc % 2) * 256 + 256]
        for l in range(self.depth):
            wk = W.next("wk%d" % l)
            self.chk("kv2")
            wkv = wk.ap[:, :4096].rearrange("p (o f) -> p o f", o=4)
            for h in range(4):
                ps = PS.get()
                self.group(ps, ps.ap[:, :256], [(wkv[:, h, kc * 128:(kc + 1) * 128], memb(kc)[1], [wk, memb(kc)[0]]) for kc in range(8)])
                self.act(self.KT[l], self.KT[l].ap[:, h, :], ps, ps.ap[:, :256], AF.Copy)
                st = F.get()
                self.cp(st, st.ap[:, :256], ps, ps.ap[:, :256])
                PS.put(ps)
                fw.dma("sp", self.o_kT[s, l, h], st.ap[:, :256], reads=[st])
                F.put(st)
            W.release(wk)
            self.chk("kv3")
            wv = W.next("wv%d" % l)
            wvv = wv.ap[:, :4096].rearrange("p (k n) -> p k n", k=8)
            for mc in range(2):
                ps = PS.get()
                self.group(ps, ps.ap[:, :512], [(memb(kc)[1][:, mc * 128:(mc + 1) * 128], wvv[:, kc, :], [wv, memb(kc)[0]]) for kc in range(8)])
                self.act(self.VV[l], self.VV[l].ap[:, mc, :], ps, ps.ap[:, :512], AF.Copy)
                st = F.get()
                self.cp(st, st.ap[:, :512], ps, ps.ap[:, :512])
                PS.put(ps)
                fw.dma("sp", self.o_v[s, l, mc * 128:(mc + 1) * 128, :], st.ap[:, :512], reads=[st])
                F.put(st)
            W.release(wv)
        H.put(*mb)

    def layer(self, l, N, last, sidx, is_sample):
        fw, F, H, PS, W = self.fw, self.F, self.H, self.PS, self.W
        X, XB, XAH, UH = self.X, self.XB, self.XAH, self.UH
        HA, HB, HST, KT, VV, OA, OB = self.HA, self.HB, self.HST, self.KT, self.VV, self.OA, self.OB
        ta, tb = min(3, N), min(30, N)

        def wview(w, no, fdim):
            return w.ap[:, :no * fdim].rearrange("p (o f) -> p o f", o=no)

        def xgroup(ps, wv, oi, rhs_bufs, w, nk=8):
            self.group(ps, ps.ap[:, :N], [(wv[:, oi, kc * 128:(kc + 1) * 128], rhs_bufs[kc].ap[:, :N], [w, rhs_bufs[kc]]) for kc in range(nk)])

        fw.op("pool", lambda e: e.tensor_copy(out=self.XAH_t[:, :, 1:4], in_=HA[l].ap), reads=[HA[l]], writes=XAH)
        fw.op("pool", lambda e: e.tensor_copy(out=self.UH_t[:, :, 2:32], in_=HB[l].ap), reads=[HB[l]], writes=UH)
        for g in range(2):
            w = W.next("win%d.%d" % (l, g)); wv = wview(w, 4, 1024)
            for oi in range(4):
                c = 4 * g + oi
                ps = PS.get(); xgroup(ps, wv, oi, XB, w)
                self.act(XAH[c], XAH[c].ap[:, HA_OFF:HA_OFF + N], ps, ps.ap[:, :N], AF.Copy)
                if last:
                    self.act(OA[l], OA[l].ap[:, c, 3 - ta:3], ps, ps.ap[:, N - ta:N], AF.Copy)
                PS.put(ps)
                self.pump(2)
            W.release(w)
        YG = []
        for g in range(2):
            w = W.next("win%d.%d" % (l, 2 + g)); wv = wview(w, 4, 1024)
            for oi in range(4):
                ps = PS.get(); xgroup(ps, wv, oi, XB, w)
                y = F.get(); self.act(y, y.ap[:, :N], ps, ps.ap[:, :N], AF.Gelu_apprx_tanh); YG.append(y)
                PS.put(ps)
                self.pump(2)
            W.release(w)
        w1 = W.next("win%d.4" % l); w2 = W.next("win%d.5" % l)
        wv1, wv2 = wview(w1, 4, 1024), wview(w2, 4, 1024)
        for c in range(4):
            p1 = PS.get(); xgroup(p1, wv1, c, XB, w1)
            p2 = PS.get(); xgroup(p2, wv2, c, XB, w2)
            sg = F.get(); self.act(sg, sg.ap[:, :N], p2, p2.ap[:, :N], AF.Sigmoid)
            PS.put(p2)
            self.tt(UH[c], UH[c].ap[:, HB_OFF:HB_OFF + N], p1, p1.ap[:, :N], sg, sg.ap[:, :N], ALU.mult)
            if last:
                self.tt(OB[l], OB[l].ap[:, c, 30 - tb:30], p1, p1.ap[:, N - tb:N], sg, sg.ap[:, N - tb:N], ALU.mult)
            PS.put(p1); F.put(sg)
            self.pump(2)
        W.release(w1); W.release(w2)
        Q = []
        w = W.next("win%d.6" % l); wv = wview(w, 4, 1024)
        for h in range(4):
            ps = PS.get(); xgroup(ps, wv, h, XB, w)
            q = H.get(); self.cp(q, q.ap[:, :N], ps, ps.ap[:, :N]); Q.append(q)
            PS.put(ps)
        W.release(w)
        fw.op("pool", lambda e: e.tensor_copy(out=HA[l].ap, in_=self.XAH_t[:, :, N + 1:N + 4]), reads=XAH, writes=[HA[l]])
        fw.op("pool", lambda e: e.tensor_copy(out=HB[l].ap, in_=self.UH_t[:, :, N + 2:N + 32]), reads=UH, writes=[HB[l]])
        if last:
            fw.dma("act", self.o_ca[sidx, l], OA[l].ap, reads=[OA[l]])
            fw.dma("act", self.o_cb[sidx, l], OB[l].ap, reads=[OB[l]])

        self.chk("win")
        wd = W.next("da%d" % l); wdv = wd.ap[:, :4096].rearrange("p (a b) -> p a b", a=32)
        wl = W.next("lru%d" % l); wlv = wl.ap[:, :2048].rearrange("p (g n d) -> p g n d", g=2, n=8)
        HG = [None] * 8
        V = [None] * 4
        st = {}

        cb = {}

        def conv_b0(c):
            wdb = W.next("db%d.%d" % (l, c)); wdbv = wdb.ap[:, :31 * 128].rearrange("p (a b) -> p a b", a=31)
            pv = PS.get()
            for k in range(16):
                self.mm(pv, pv.ap[:, :N], wdbv[:, k, :], UH[c].ap[:, 2 + k:2 + k + N], k == 0, False, [wdb, UH[c]])
            cb[c] = (wdb, wdbv, pv)

        def conv_b1(c):
            wdb, wdbv, pv = cb.pop(c)
            for k in range(16, 31):
                self.mm(pv, pv.ap[:, :N], wdbv[:, k, :], UH[c].ap[:, 2 + k:2 + k + N], False, k == 30, [wdb, UH[c]])
            v = F.get(); self.act(v, v.ap[:, :N], pv, pv.ap[:, :N], AF.Identity, bias=self.cc("conv_b_b%d" % l, c), extra=[self.C]); V[c] = v
            PS.put(pv); W.release(wdb)

        def s1(c):
            pc = PS.get()
            self.group(pc, pc.ap[:, :N], [(wdv[:, k * 8 + c, :], XAH[c].ap[:, 1 + k:1 + k + N], [wd, XAH[c]]) for k in range(4)])
            xc = F.get(); self.act(xc, xc.ap[:, :N], pc, pc.ap[:, :N], AF.Identity, bias=self.cc("conv_a_b%d" % l, c), extra=[self.C])
            PS.put(pc)
            xcb = H.get(); self.cp(xcb, xcb.ap[:, :N], xc, xc.ap[:, :N])
            st[c] = {"xc": xc, "xcb": xcb}

        def s2_act(c):
            d = st[c]; xcb = d.pop("xcb")
            pr = PS.get(); self.group(pr, pr.ap[:, :N], [(wlv[:, 0, c, :], xcb.ap[:, :N], [wl, xcb])])
            pi = PS.get(); self.group(pi, pi.ap[:, :N], [(wlv[:, 1, c, :], xcb.ap[:, :N], [wl, xcb])])
            H.put(xcb)
            r = F.get(); self.act(r, r.ap[:, :N], pr, pr.ap[:, :N], AF.Tanh, scale=0.5, bias=self.ch("lru_b_r%d" % l, c), extra=[self.CH]); PS.put(pr)
            i = F.get(); self.act(i, i.ap[:, :N], pi, pi.ap[:, :N], AF.Tanh, scale=0.5, bias=self.ch("lru_b_i%d" % l, c), extra=[self.CH]); PS.put(pi)
            h8 = self.C16.ap[:, l, c:c + 1]
            a = F.get(); self.act(a, a.ap[:, :N], r, r.ap[:, :N], AF.Exp, scale=h8, bias=h8, extra=[self.C16])
            d.update(r=r, i=i, a=a)

        def s2_T(c):
            d = st[c]; r, a = d["r"], d["a"]
            self.tt(r, r.ap[:, :N], a, a.ap[:, :N], a, a.ap[:, :N], ALU.mult)

        def s2_S(c):
            r = st[c]["r"]
            self.act(r, r.ap[:, :N], r, r.ap[:, :N], AF.Sqrt, scale=-0.25, bias=0.25)

        def s2_ix(c):
            d = st[c]; i, xc = d["i"], d.pop("xc")
            fw.op("dve", lambda e: e.scalar_tensor_tensor(out=i.ap[:, :N], in0=i.ap[:, :N], scalar=1.0, in1=xc.ap[:, :N], op0=ALU.add, op1=ALU.mult),
                  reads=[i, xc], writes=[i])
            F.put(xc)

        def s2_dve(c):
            d = st.pop(c); r, i, a = d["r"], d["i"], d["a"]
            self.tt(i, i.ap[:, :N], i, i.ap[:, :N], r, r.ap[:, :N], ALU.mult); F.put(r)
            hh = F.get()
            hst = HST[l]
            fw.op("dve", lambda e: e.tensor_tensor_scan(out=hh.ap[:, :N], data0=a.ap[:, :N], data1=i.ap[:, :N], initial=hst.ap[:, c:c + 1], op0=ALU.mult, op1=ALU.add),
                  reads=[a, i, hst], writes=[hh])
            F.put(a); F.put(i)
            self.cp(hst, hst.ap[:, c:c + 1], hh, hh.ap[:, N - 1:N])
            hg = H.get(); self.tt(hg, hg.ap[:, :N], hh, hh.ap[:, :N], YG[c], YG[c].ap[:, :N], ALU.mult)
            HG[c] = hg
            F.put(hh); F.put(YG[c])
        s1(0)
        s1(1)
        OT = [None] * 4
        kt, vvb = KT[l], VV[l]
        cst = {}

        def c_scores(h):
            pss = [PS.get(), PS.get()]
            pt = [H.get(), H.get()]
            for mc in range(2):
                self.group(pss[mc], pss[mc].ap[:, :N], [(kt.ap[:, h, mc * 128:(mc + 1) * 128], Q[h].ap[:, :N], [kt, Q[h]])])
                self.act(pt[mc], pt[mc].ap[:, :N], pss[mc], pss[mc].ap[:, :N], AF.Exp, scale=SCALE)
                PS.put(pss[mc])
            H.put(Q[h])
            cst[h] = pt

        def c_pv(h):
            pt = cst.pop(h)
            psum_s = PS.get()
            self.group(psum_s, psum_s.ap[:, :N], [(self.ONES.ap, pt[mc].ap[:, :N], [self.ONES, pt[mc]]) for mc in range(2)])
            po = PS.get()
            self.group(po, po.ap[:, :N], [(vvb.ap[:, mc, h * 128:(h + 1) * 128], pt[mc].ap[:, :N], [vvb, pt[mc]]) for mc in range(2)])
            rs = F.get()
            fw.op("dve", lambda e: e.reciprocal(out=rs.ap[:, :N], in_=psum_s.ap[:, :N]), reads=[psum_s], writes=[rs])
            PS.put(psum_s)
            ot = H.get(); OT[h] = ot
            self.tt(ot, ot.ap[:, :N], po, po.ap[:, :N], rs, rs.ap[:, :N], ALU.mult)
            PS.put(po); F.put(rs); H.put(*pt)
        c_scores(0)
        for h in range(4):
            if h + 1 < 4:
                c_scores(h + 1)
            c_pv(h)

        conv_b0(0)
        conv_b1(0)
        for it in range(10):
            if it + 2 < 8:
                s1(it + 2)
            if it < 8:
                s2_act(it)
            if it % 2 == 0 and it >= 2:
                s2_S(it - 2)
                s2_S(it - 1)
            if it < 8:
                s2_T(it)
                s2_ix(it)
            if it % 2 == 0 and it >= 2:
                s2_dve(it - 2)
                s2_dve(it - 1)
            if it < 6:
                if it % 2 == 0:
                    conv_b0(1 + it // 2)
                else:
                    conv_b1(1 + it // 2)
        W.release(wd); W.release(wl)
        if last:
            fw.dma("act", self.o_lru[sidx, l], HST[l].ap, reads=[HST[l]])
        self.chk("brA")

        YB = [None] * 4

        def dst_b(c):
            YB[c] = H.get()
            return YB[c], YB[c].ap[:, :N]
        self.chk("lnB")
        lnb = deque()
        self.layernorm(V, 4, N, self.ONES512, "ln_b_g%d" % l, "ln_b_b%d" % l, AF.Silu, dst_b, defer_to=lnb,
                       on_done=lambda: F.put(*V))

        def pump_lnb(n):
            while n > 0 and lnb:
                lnb.popleft()()
                n -= 1
        pump_lnb(2)

        M = [None] * 8
        MB = [None] * 8

        def gated_proj(gslot0, k, pname, nk, rhs_bufs, combine):
            if nk == 8:
                for g in range(2):
                    wg = W.next("win%d.%d" % (l, gslot0 + g)); wgv = wview(wg, 4, 1024)
                    wp = W.next("%s%d.%d" % (pname, l, 4 * g)); wpv = wview(wp, 4, 1024)
                    for oi in range(4):
                        o = 4 * g + oi
                        pg = PS.get(); xgroup(pg, wgv, oi, XB, wg)
                        po = PS.get(); xgroup(po, wpv, oi, rhs_bufs, wp)
                        G = F.get(); self.act(G, G.ap[:, :N], pg, pg.ap[:, :N], AF.Tanh, scale=0.5, bias=self.ch("b_gate%d" % l, k * 8 + o), extra=[self.CH]); PS.put(pg)
                        combine(o, G, po)
                        PS.put(po); F.put(G)
                        pump_lnb(2)
                    W.release(wg); W.release(wp)
            else:
                wp = W.next("%s%d.0" % (pname, l)); wpv = wview(wp, 8, 512)
                for g in range(2):
                    wg = W.next("win%d.%d" % (l, gslot0 + g)); wgv = wview(wg, 4, 1024)
                    for oi in range(4):
                        o = 4 * g + oi
                        pg = PS.get(); xgroup(pg, wgv, oi, XB, wg)
                        po = PS.get(); xgroup(po, wpv, o, rhs_bufs, wp, nk=4)
                        G = F.get(); self.act(G, G.ap[:, :N], pg, pg.ap[:, :N], AF.Tanh, scale=0.5, bias=self.ch("b_gate%d" % l, k * 8 + o), extra=[self.CH]); PS.put(pg)
                        combine(o, G, po)
                        PS.put(po); F.put(G)
                        pump_lnb(2)
                    W.release(wg)
                W.release(wp)

        def comb_a(o, G, po):
            m = F.get(); M[o] = m
            self.tt(m, m.ap[:, :N], po, po.ap[:, :N], G, G.ap[:, :N], ALU.mult)

        def comb_b(o, G, po):
            t = F.get()
            self.tt(t, t.ap[:, :N], po, po.ap[:, :N], G, G.ap[:, :N], ALU.mult)
            self.tt(M[o], M[o].ap[:, :N], M[o], M[o].ap[:, :N], t, t.ap[:, :N], ALU.add)
            F.put(t)

        def comb_c(o, G, po):
            t = F.get()
            self.tt(t, t.ap[:, :N], po, po.ap[:, :N], G, G.ap[:, :N], ALU.mult)
            mb = H.get(); MB[o] = mb
            self.tt(mb, mb.ap[:, :N], M[o], M[o].ap[:, :N], t, t.ap[:, :N], ALU.add)
            F.put(t); F.put(M[o])
        def gmul(out_b, G, po):
            fw.op("dve", lambda e: e.scalar_tensor_tensor(out=out_b.ap[:, :N], in0=G.ap[:, :N], scalar=1.0, in1=po.ap[:, :N], op0=ALU.add, op1=ALU.mult),
                  reads=[G, po], writes=[out_b])

        def comb_first(o, G, po):
            m = F.get(); M[o] = m
            gmul(m, G, po)

        def comb_mid(o, G, po):
            t = F.get()
            gmul(t, G, po)
            self.tt(M[o], M[o].ap[:, :N], M[o], M[o].ap[:, :N], t, t.ap[:, :N], ALU.add)
            F.put(t)

        def comb_last(o, G, po):
            t = F.get()
            gmul(t, G, po)
            mb = H.get(); MB[o] = mb
            self.tt(mb, mb.ap[:, :N], M[o], M[o].ap[:, :N], t, t.ap[:, :N], ALU.add)
            F.put(t); F.put(M[o])
        self.chk("projC")
        gated_proj(11, 2, "pc", 4, OT, comb_first)
        H.put(*OT)
        pump_lnb(100)
        self.chk("projB")
        gated_proj(9, 1, "pb", 4, YB, comb_mid)
        H.put(*YB)
        self.chk("projA")
        gated_proj(7, 0, "pa", 8, HG, comb_last)
        H.put(*HG)
        self.chk("wout")

        R1 = [None] * 8

        def resid(o, po, alpha=ALPHA):
            r1 = F.get(); R1[o] = r1
            fw.op("dve", lambda e: e.scalar_tensor_tensor(out=r1.ap[:, :N], in0=X[o].ap[:, :N], scalar=alpha, in1=po.ap[:, :N], op0=ALU.mult, op1=ALU.add),
                  reads=[X[o], po], writes=[r1])
        for g in range(2):
            w = W.next("wo%d.%d" % (l, 4 * g)); wv = wview(w, 4, 1024)
            for oi in range(4):
                o = 4 * g + oi
                po = PS.get(); xgroup(po, wv, oi, MB, w)
                resid(o, po, 2.0 * ALPHA)
                PS.put(po)
            W.release(w)
        H.put(*MB)

        def dst_x(c):
            return X[c], X[c].ap[:, :N]

        def post_x(c):
            self.cp(XB[c], XB[c].ap[:, :N], X[c], X[c].ap[:, :N], eng="pool")
        self.chk("ln1_start")
        r1a = list(R1)
        self.layernorm(r1a, 8, N, self.ONES1K, "ln1_g%d" % l, "ln1_b%d" % l, AF.Identity, dst_x, post_x,
                       defer=True, on_done=lambda: F.put(*r1a), eps=4.0 * LN_EPS)

        yield
        self.chk("ln1")
        HF = [None] * NFF
        for s in range(6):
            no = 4 if s < 5 else 2
            wg = W.next("fg%d.%d" % (l, 4 * s)); wgv = wview(wg, no, 1024)
            wu = W.next("fu%d.%d" % (l, 4 * s)); wuv = wview(wu, no, 1024)
            for oi in range(no):
                j = 4 * s + oi
                pg = PS.get(); xgroup(pg, wgv, oi, XB, wg)
                pu = PS.get(); xgroup(pu, wuv, oi, XB, wu)
                sg = F.get(); self.act(sg, sg.ap[:, :N], pg, pg.ap[:, :N], AF.Silu); PS.put(pg)
                hf = H.get(); HF[j] = hf
                self.tt(hf, hf.ap[:, :N], pu, pu.ap[:, :N], sg, sg.ap[:, :N], ALU.mult)
                PS.put(pu); F.put(sg)
                self.pump(2)
            W.release(wg); W.release(wu)
        self.chk("ffn_down")
        for o in range(8):
            w = W.next("fd%d.%d" % (l, o)); wv = w.ap[:, :2816].rearrange("p (o f) -> p o f", o=1)
            po = PS.get()
            self.group(po, po.ap[:, :N], [(wv[:, 0, kc * 128:(kc + 1) * 128], HF[kc].ap[:, :N], [w, HF[kc]]) for kc in range(NFF)])
            resid(o, po)
            PS.put(po)
            self.pump(2)
            W.release(w)
        H.put(*HF)
        self.chk("ln2")
        final = (l == self.depth - 1)
        r1b = list(R1)
        self.layernorm(r1b, 8, N, self.ONES1K, "ln2_g%d" % l, "ln2_b%d" % l, AF.Identity, dst_x, None if final else post_x,
                       defer=True, on_done=lambda: F.put(*r1b))

    def tile_begin(self, src_ap, N):
        fw = self.fw
        X, XB = self.X, self.XB
        fw.dma("sp", self.X_t[:, :, :N], src_ap.rearrange("(c p) n -> p c n", p=128), writes=X)

        def dst_x(c):
            return X[c], X[c].ap[:, :N]

        def post_x(c):
            self.cp(XB[c], XB[c].ap[:, :N], X[c], X[c].ap[:, :N], eng="pool")
        self.layernorm(X, 8, N, self.ONES1K, "ln_in_g", "ln_in_b", AF.Identity, dst_x, post_x)
        self.chk("ln_in")

    def tile_end(self, dst_ap, N):
        self.fw.dma("act", dst_ap.rearrange("(c p) n -> p c n", p=128), self.X_t[:, :, :N], reads=self.X)

    def build(self):
        fw = self.fw
        try:
            self._build()
        except StopBuild:
            pass
        fw.wait_all_dma("pool")
        fw.finish()
        return self.nc

    def _build(self):
        fw = self.fw
        self.phase0()
        self.chk("phase0")
        self.schedule_all()
        TN = self.TN
        done_kv = set()
        for rnd in self.rounds():
            infos = []
            for kind, slot, s, t in rnd:
                self.use_ctx(slot)
                self.use_seq(slot)
                if kind == "p":
                    if s not in done_kv:
                        done_kv.add(s)
                        for l in range(self.depth):
                            fw.op("pool", lambda e, b=self.HA[l]: e.memset(b.ap, 0.0), writes=[self.HA[l]])
                            fw.op("pool", lambda e, b=self.HB[l]: e.memset(b.ap, 0.0), writes=[self.HB[l]])
                            fw.op("pool", lambda e, b=self.HST[l]: e.memset(b.ap, 0.0), writes=[self.HST[l]])
                        self.chk("kv0")
                        self.kv_phase(s)
                        self.chk("kv")
                    N = TN
                    last = (t == self.seq // TN - 1)
                    src = self.xT[s, :, t * TN:(t + 1) * TN]
                    dst = self.yT[s, :, t * TN:(t + 1) * TN]
                else:
                    smp = Buf(None, "smp", dgroup="SMP")
                    for l in range(self.depth):
                        fw.dma("pool", self.HA[l].ap, self.sca[:, l], writes=[self.HA[l]], sem_buf=smp)
                        fw.dma("pool", self.HB[l].ap, self.scb[:, l], writes=[self.HB[l]], sem_buf=smp)
                        fw.dma("pool", self.HST[l].ap, self.slru[:, l], writes=[self.HST[l]], sem_buf=smp)
                        fw.dma("pool", self.KT[l].ap, self.ckT[:, l], writes=[self.KT[l]], sem_buf=smp)
                        fw.dma("pool", self.VV[l].ap, self.cv[:, l], writes=[self.VV[l]], sem_buf=smp)
                        fw.dma("pool", self.OB[l].ap[:, :, 0:14], self.scb[:, l, :, 16:30], writes=[self.OB[l]], sem_buf=smp)
                    N, last = 16, True
                    src = self.xsT
                    dst = self.ysT
                self.tile_begin(src, N)
                infos.append((slot, N, last, s, kind == "s", dst))
            for l in range(self.depth):
                if l >= 1:
                    while self.bg_casts:
                        self.bg_casts.popleft()()
                gens = []
                for slot, N, last, s, is_s, dst in infos:
                    self.use_ctx(slot)
                    self.use_seq(slot)
                    self.chk("xa")
                    self.flush(slot)
                    g = self.layer(l, N, last, s, is_s)
                    next(g)
                    gens.append((slot, g))
                for slot, g in gens:
                    self.use_ctx(slot)
                    self.use_seq(slot)
                    self.flush(slot)
                    for _ in g:
                        pass
            self.chk("tile_out")
            for slot, N, last, s, is_s, dst in infos:
                self.use_ctx(slot)
                self.flush(slot)
                self.tile_end(dst, N)


def assemble(results, nseq, seq, with_sample=True):
    n = len(results)
    L = DEPTH
    y = np.concatenate([r["yT"].transpose(0, 2, 1) for r in results], axis=0)
    ys = np.stack([r["ysT"].T for r in results], axis=0)

    def ca(idx):
        a = np.concatenate([r["o_ca"][idx] for r in results], axis=0)
        return np.ascontiguousarray(a.transpose(1, 0, 4, 3, 2)).reshape(L, a.shape[0], 3, 1024)

    def lru(idx):
        a = np.concatenate([r["o_lru"][idx] for r in results], axis=0)
        return np.ascontiguousarray(a.transpose(1, 0, 3, 2)).reshape(L, a.shape[0], 1024)

    def cb(idx):
        a = np.concatenate([r["o_cb"][idx] for r in results], axis=0)
        return np.ascontiguousarray(a.transpose(1, 0, 4, 3, 2)).reshape(L, a.shape[0], 30, 512)
    pi = slice(0, nseq)
    si = slice(nseq, nseq + 1)
    kT = np.concatenate([r["o_kT"] for r in results], axis=0)
    mk = np.ascontiguousarray(kT.transpose(1, 0, 4, 2, 3))
    v = np.concatenate([r["o_v"] for r in results], axis=0)
    mv = np.ascontiguousarray(v.transpose(1, 0, 2, 3)).reshape(L, v.shape[0], N_MEM, 4, 128)
    return (y, ys, ca(pi), lru(pi), cb(pi), mk, mv, ca(si), lru(si), cb(si))


_CACHE = {}


def kernel(**inputs):
    n = 8
    nseq, seq = 2, 4096
    inp = {k: np.asarray(v) for k, v in inputs.items()}
    shared = prep_shared(inp)
    in_maps = []
    for i in range(n):
        m = prep_core(inp, i, nseq, seq)
        m.update(shared)
        in_maps.append(m)
    nc = Builder(nseq, seq).build()
    res = run_bass_kernel_spmd(nc, in_maps, core_ids=list(range(n)))
    return assemble(res.results, nseq, seq)
```

```python
import numpy as np
from collections import deque
import concourse.bass as bass
import concourse.mybir as mybir
from concourse.bass_utils import run_bass_kernel_spmd

F32 = mybir.dt.float32
BF16 = mybir.dt.bfloat16
AF = mybir.ActivationFunctionType
ALU = mybir.AluOpType

D = 1024
DEPTH = 4
D_IN = 6656
D_FF = 2816
NFF = 22
N_MEM = 256
ALPHA = float((2 * DEPTH) ** 0.25)
LN_EPS = 1e-5
SCALE = float(128 ** -0.5)
HA_OFF = 4
HB_OFF = 32

ENGS = ("pe", "act", "dve", "pool", "sp")


class Buf:
    __slots__ = ("ap", "w", "r", "name", "dsem", "psum", "dgroup")

    def __init__(self, ap, name="", psum=False, dgroup=None):
        self.ap = ap
        self.psum = psum
        self.dgroup = dgroup
        self.w = None
        self.r = {}
        self.name = name
        self.dsem = None


class FW:
    def __init__(self, nc):
        self.nc = nc
        self.prog = {e: [] for e in ENGS}
        self.count = {e: 0 for e in ENGS}
        self.seen = {e: {} for e in ENGS}
        self.sems = {}
        self.dcount = {}
        self.n_dsem = 0
        self.pool_inflight = deque()
        self.shared = set()
        self.groups = {}
        self.phase = ""
        self.labels = {e: [] for e in ENGS}
        for e in ENGS:
            self.sems[e] = nc.alloc_semaphore("sem_" + e)

    def _deps(self, eng, reads, writes):
        deps = {}

        def add(k, v):
            if deps.get(k, 0) < v:
                deps[k] = v
        for b in reads:
            if b.w is not None:
                add(*b.w)
            if b.psum:
                for k, v in b.r.items():
                    if k != eng:
                        add(k, v)
        for b in writes:
            if b.w is not None and b.w[0] != eng:
                add(*b.w)
            for k, v in b.r.items():
                if k != eng:
                    add(k, v)
        waits = []
        seen = self.seen[eng]
        for k, v in deps.items():
            if seen.get(k, 0) < v:
                seen[k] = v
                waits.append((k, v))
        return waits

    def op(self, eng, fn, reads=(), writes=(), signal=True):
        waits = self._deps(eng, reads, writes)
        if signal:
            self.count[eng] += 1
            tk = (eng, self.count[eng])
        else:
            tk = (eng, self.count[eng] + 1)
        for b in writes:
            b.w = tk
            b.r = {}
        for b in reads:
            if b.r.get(eng, 0) < tk[1]:
                b.r[eng] = tk[1]
        self.prog[eng].append((waits, fn, eng if signal else None, 1))
        self.labels[eng].append(self.phase)
        return tk

    def dsem_for(self, b):
        if b.dsem is None and b.dgroup is not None:
            if b.dgroup not in self.groups:
                k = "g_" + b.dgroup
                self.n_dsem += 1
                self.sems[k] = self.nc.alloc_semaphore("dsem_" + b.dgroup)
                self.dcount[k] = 0
                self.groups[b.dgroup] = k
                self.shared.add(k)
            b.dsem = self.groups[b.dgroup]
        if b.dsem is None:
            k = "d%d" % self.n_dsem
            self.n_dsem += 1
            self.sems[k] = self.nc.alloc_semaphore("dsem_%d" % self.n_dsem)
            self.dcount[k] = 0
            b.dsem = k
        return b.dsem

    def dma(self, q, out, in_, reads=(), writes=(), sem_buf=None, max_inflight=6):
        if sem_buf is None:
            sem_buf = (list(writes) + list(reads))[0]
        k = self.dsem_for(sem_buf)
        waits = self._deps(q, reads, writes)
        if k in self.shared and self.dcount[k] > 0 and self.seen[q].get(k, 0) < self.dcount[k]:
            self.seen[q][k] = self.dcount[k]
            waits.append((k, self.dcount[k]))
        if q == "pool":
            while len(self.pool_inflight) >= max_inflight:
                ok, ov = self.pool_inflight.popleft()
                if self.seen[q].get(ok, 0) < ov:
                    self.seen[q][ok] = ov
                    waits.append((ok, ov))
        self.dcount[k] += 16
        tk = (k, self.dcount[k])
        for b in writes:
            b.w = tk
            b.r = {}
        for b in reads:
            b.r[k] = tk[1]
        if q == "pool":
            self.pool_inflight.append(tk)
        self.prog[q].append((waits, lambda e: e.dma_start(out=out, in_=in_), k, 16))
        return tk

    def wait_all_dma(self, q):
        waits = [(k, v) for k, v in self.dcount.items() if v > 0]
        self.prog[q].append((waits, None, None, 0))

    def finish(self):
        nc = self.nc
        with nc.Block() as block:
            def mk(eng):
                def body(e):
                    for waits, fn, sk, inc in self.prog[eng]:
                        for k, v in waits:
                            e.wait_ge(self.sems[k], v)
                        if fn is not None:
                            ins = fn(e)
                            if sk is not None:
                                ins.then_inc(self.sems[sk], inc)
                return body
            block.tensor(mk("pe"))
            block.scalar(mk("act"))
            block.vector(mk("dve"))
            block.gpsimd(mk("pool"))
            block.sync(mk("sp"))


class Pool:
    def __init__(self, bufs, name):
        self.free = deque(bufs)
        self.name = name
        self.low = len(bufs)

    def get(self):
        if not self.free:
            raise RuntimeError("pool %s exhausted" % self.name)
        b = self.free.popleft()
        self.low = min(self.low, len(self.free))
        return b

    def put(self, *bs):
        for b in bs:
            self.free.append(b)


class WStream:
    def __init__(self, fw, slots, q="sp"):
        self.fw = fw
        self.free = deque(slots)
        self.pending = deque()
        self.loaded = deque()
        self.q = q

    def schedule(self, name, src_ap, per_part, src_buf):
        self.pending.append((name, src_ap, per_part, src_buf))

    def _pump(self):
        while self.free and self.pending:
            name, src_ap, per_part, src_buf = self.pending.popleft()
            slot = self.free.popleft()
            dst = slot.ap[:, :per_part]
            if len(src_ap.shape) == 3:
                dst = dst.rearrange("p (o f) -> p o f", o=src_ap.shape[1])
            self.fw.dma(self.q, dst, src_ap, reads=[src_buf], writes=[slot], sem_buf=slot)
            self.loaded.append((name, slot))

    def next(self, name):
        self._pump()
        nm, slot = self.loaded.popleft()
        assert nm == name, (nm, name)
        return slot

    def release(self, slot):
        self.free.append(slot)
        self._pump()


def const_layout():
    off = {}
    n = 0

    def add(name, w):
        nonlocal n
        off[name] = n
        n += w
    add("ln_in_g", 8)
    add("ln_in_b", 8)
    for l in range(DEPTH):
        add("b_gate%d" % l, 24)
        add("conv_a_b%d" % l, 8)
        add("lru_b_r%d" % l, 8)
        add("lru_b_i%d" % l, 8)
        add("lru_lambda%d" % l, 8)
        add("conv_b_b%d" % l, 4)
        add("ln_b_g%d" % l, 4)
        add("ln_b_b%d" % l, 4)
        add("ln1_g%d" % l, 8)
        add("ln1_b%d" % l, 8)
        add("ln2_g%d" % l, 8)
        add("ln2_b%d" % l, 8)
        add("conv_a_w%d" % l, 32)
        add("conv_b_w%d" % l, 124)
    return off, n


COFF, NCONST = const_layout()

WSHAPES = {
    "w_in_t": (DEPTH * 52 * 128, 1024),
    "proj_a_t": (DEPTH * 8 * 128, 1024),
    "proj_b_t": (DEPTH * 8 * 128, 512),
    "proj_c_t": (DEPTH * 8 * 128, 512),
    "w_out_t": (DEPTH * 8 * 128, 1024),
    "lru_w": (DEPTH * 128, 2048),
    "wk_t": (DEPTH * 4 * 128, 1024),
    "wv_m": (DEPTH * 128, 4096),
    "wg_t": (DEPTH * NFF * 128, 1024),
    "wu_t": (DEPTH * NFF * 128, 1024),
    "wd_t": (DEPTH * 8 * 128, 2816),
}


def _tile_w(w):
    L, K, Nout = w.shape
    t = w.reshape(L, K // 128, 128, Nout // 128, 128)
    t = t.transpose(0, 3, 2, 1, 4)
    return np.ascontiguousarray(t).reshape(L * (Nout // 128) * 128, K)


def _pcol(v, nch):
    v = np.asarray(v)
    lead = v.shape[:-1]
    t = v.reshape(lead + (nch, 128))
    t = np.moveaxis(t, -1, 0)
    return np.ascontiguousarray(t)


def prep_shared(inp):
    sh = {}
    sh["w_in_t"] = _tile_w(inp["w_in"])
    sh["proj_a_t"] = _tile_w(inp["proj_a"])
    sh["proj_b_t"] = _tile_w(inp["proj_b"])
    sh["proj_c_t"] = _tile_w(inp["proj_c"])
    sh["w_out_t"] = _tile_w(inp["w_out"])
    sh["wk_t"] = _tile_w(inp["w_mem_k"])
    sh["wg_t"] = _tile_w(inp["w_ffn_gate"])
    sh["wu_t"] = _tile_w(inp["w_ffn_up"])
    sh["wd_t"] = _tile_w(inp["w_ffn_down"])
    wr = np.asarray(inp["lru_w_r"]).transpose(0, 2, 1, 3)
    wi = np.asarray(inp["lru_w_i"]).transpose(0, 2, 1, 3)
    sh["lru_w"] = np.ascontiguousarray(np.stack([wr, wi], axis=2)).reshape(DEPTH * 128, 2048)
    wv = np.asarray(inp["w_mem_v"]).reshape(DEPTH, 8, 128, 512).transpose(0, 2, 1, 3)
    sh["wv_m"] = np.ascontiguousarray(wv).reshape(DEPTH * 128, 4096)
    c = np.zeros((128, NCONST), np.float32)

    def put(name, arr):
        arr = np.asarray(arr, np.float32).reshape(128, -1)
        c[:, COFF[name]:COFF[name] + arr.shape[1]] = arr
    put("ln_in_g", _pcol(inp["ln_in_g"], 8))
    put("ln_in_b", _pcol(inp["ln_in_b"], 8))
    for l in range(DEPTH):
        put("b_gate%d" % l, _pcol(inp["b_gate"][l], 8))
        put("conv_a_b%d" % l, _pcol(inp["conv_a_b"][l], 8))
        put("lru_b_r%d" % l, _pcol(np.asarray(inp["lru_b_r"][l]).reshape(-1), 8))
        put("lru_b_i%d" % l, _pcol(np.asarray(inp["lru_b_i"][l]).reshape(-1), 8))
        put("lru_lambda%d" % l, _pcol(inp["lru_lambda"][l], 8))
        put("conv_b_b%d" % l, _pcol(inp["conv_b_b"][l], 4))
        put("ln_b_g%d" % l, _pcol(inp["ln_b_g"][l], 4))
        put("ln_b_b%d" % l, _pcol(inp["ln_b_b"][l], 4))
        put("ln1_g%d" % l, _pcol(inp["ln1_g"][l], 8))
        put("ln1_b%d" % l, _pcol(inp["ln1_b"][l], 8))
        put("ln2_g%d" % l, _pcol(inp["ln2_g"][l], 8))
        put("ln2_b%d" % l, _pcol(inp["ln2_b"][l], 8))
        put("conv_a_w%d" % l, _pcol(inp["conv_a_w"][l], 8))
        cb = _pcol(inp["conv_b_w"][l], 4)
        put("conv_b_w%d" % l, np.ascontiguousarray(cb.transpose(0, 2, 1)))
    sh["consts"] = c
    return sh


def prep_core(inp, i, nseq, seq):
    m = {}
    xp = np.asarray(inp["x_prompt"])
    m["xT"] = np.ascontiguousarray(xp[nseq * i:nseq * (i + 1), :seq].transpose(0, 2, 1))
    m["xsT"] = np.ascontiguousarray(np.asarray(inp["x_sample"])[i].T)
    m["memT"] = np.ascontiguousarray(np.asarray(inp["mem_prompt"])[nseq * i:nseq * (i + 1)].transpose(0, 2, 1))
    sca = np.asarray(inp["state_conv_a"])[:, i]
    m["sca"] = np.ascontiguousarray(_pcol(sca, 8).transpose(0, 1, 3, 2))
    m["slru"] = _pcol(np.asarray(inp["state_lru"])[:, i], 8)
    scb = np.asarray(inp["state_conv_b"])[:, i]
    m["scb"] = np.ascontiguousarray(_pcol(scb, 4).transpose(0, 1, 3, 2))
    ck = np.asarray(inp["cache_mem_k"])[:, i]
    m["ckT"] = np.ascontiguousarray(ck.transpose(3, 0, 2, 1))
    cv = np.asarray(inp["cache_mem_v"])[:, i].reshape(DEPTH, 2, 128, 512)
    m["cv"] = np.ascontiguousarray(cv.transpose(2, 0, 1, 3))
    return m


class StopBuild(Exception):
    pass


class Builder:
    stop_at = None

    def chk(self, name):
        self.fw.phase = name
        if self.stop_at == name:
            raise StopBuild(name)

    def use_ctx(self, i):
        self.X, self.XB, self.X_t = self.ctx_bufs[i]
        self.cur_slot = i

    def pump(self, n=2):
        for slot, dq in self.deferred.items():
            if slot == self.cur_slot:
                continue
            while n > 0 and dq:
                dq.popleft()()
                n -= 1

    def flush(self, slot):
        dq = self.deferred[slot]
        while dq:
            dq.popleft()()

    def use_seq(self, i):
        for k, v in self.seq_bufs[i].items():
            setattr(self, k, v)

    def __init__(self, nseq, seq, depth=DEPTH, tile=512, with_sample=True, n_wslots=4, NF=25, NH=28):
        self.nseq, self.seq, self.depth, self.TN, self.with_sample = nseq, seq, depth, tile, with_sample
        nc = self.nc = bass.Bass("TRN2", target_bir_lowering=False, dynamic_dma_scratch_size=8192)
        fw = self.fw = FW(nc)
        L = DEPTH
        di = lambda name, shape: nc.dram_tensor(name, list(shape), F32, kind="ExternalInput").ap()
        do = lambda name, shape: nc.dram_tensor(name, list(shape), F32, kind="ExternalOutput").ap()
        self.xT = di("xT", [nseq, D, seq])
        self.xsT = di("xsT", [D, 16])
        self.memT = di("memT", [nseq, D, N_MEM])
        self.sca = di("sca", [128, L, 8, 3])
        self.slru = di("slru", [128, L, 8])
        self.scb = di("scb", [128, L, 4, 30])
        self.ckT = di("ckT", [128, L, 4, N_MEM])
        self.cv = di("cv", [128, L, 2, 512])
        self.consts_d = di("consts", [128, NCONST])
        self.win = {k: di(k, v) for k, v in WSHAPES.items()}
        self.wsc = {k: nc.dram_tensor("sc_" + k, list(v), BF16).ap() for k, v in WSHAPES.items()}
        self.wsc_buf = {(k, l): Buf(None, "sc_%s%d" % (k, l), dgroup="SC%d" % l) for k in WSHAPES for l in range(L)}
        self.da_sc = nc.dram_tensor("sc_da", [L * 128, 4096], BF16).ap()
        self.db_sc = nc.dram_tensor("sc_db", [L * 4 * 128, 31 * 128], BF16).ap()
        self.da_buf = [Buf(None, "sc_da%d" % l, dgroup="SC%d" % l) for l in range(L)]
        self.db_buf = [Buf(None, "sc_db%d" % l, dgroup="SC%d" % l) for l in range(L)]
        self.yT = do("yT", [nseq, D, seq])
        self.ysT = do("ysT", [D, 16])
        self.o_ca = do("o_ca", [nseq + 1, L, 128, 8, 3])
        self.o_lru = do("o_lru", [nseq + 1, L, 128, 8])
        self.o_cb = do("o_cb", [nseq + 1, L, 128, 4, 30])
        self.o_kT = do("o_kT", [nseq, L, 4, 128, N_MEM])
        self.o_v = do("o_v", [nseq, L, N_MEM, 512])

        sb = lambda name, shape, dt: nc.alloc_sbuf_tensor(name, list(shape), dt)
        TN = tile
        self.ctx_bufs = []
        for i in range(2):
            xt = sb("X%d" % i, [128, 8, TN], F32)
            xbt = sb("XB%d" % i, [128, 8, TN], BF16)
            self.ctx_bufs.append(([Buf(xt[:, c, :], "X%d_%d" % (i, c), dgroup="X%d" % i) for c in range(8)],
                                  [Buf(xbt[:, c, :], "XB%d_%d" % (i, c)) for c in range(8)], xt))
        self.XAH_t = sb("XAH", [128, 8, HA_OFF + TN], BF16)
        self.XAH = [Buf(self.XAH_t[:, c, :], "XAH%d" % c) for c in range(8)]
        self.UH_t = sb("UH", [128, 4, HB_OFF + TN], BF16)
        self.UH = [Buf(self.UH_t[:, c, :], "UH%d" % c) for c in range(4)]
        self.seq_bufs = []
        for i in range(2):
            kt = sb("KT%d" % i, [128, L, 4, N_MEM], BF16)
            vv = sb("VV%d" % i, [128, L, 2, 512], BF16)
            ha = sb("HA%d" % i, [128, L, 8, 3], BF16)
            hb = sb("HB%d" % i, [128, L, 4, 30], BF16)
            hs = sb("HST%d" % i, [128, L, 8], F32)
            oa = sb("OA%d" % i, [128, L, 8, 3], F32)
            ob = sb("OB%d" % i, [128, L, 4, 30], F32)
            self.seq_bufs.append(dict(
                KT=[Buf(kt[:, l], "KT%d_%d" % (i, l), dgroup="S%d" % i) for l in range(L)],
                VV=[Buf(vv[:, l], "VV%d_%d" % (i, l), dgroup="S%d" % i) for l in range(L)],
                HA=[Buf(ha[:, l], "HA%d_%d" % (i, l), dgroup="S%d" % i) for l in range(L)],
                HB=[Buf(hb[:, l], "HB%d_%d" % (i, l), dgroup="S%d" % i) for l in range(L)],
                HST=[Buf(hs[:, l], "HST%d_%d" % (i, l), dgroup="S%d" % i) for l in range(L)],
                OA=[Buf(oa[:, l], "OA%d_%d" % (i, l), dgroup="S%d" % i) for l in range(L)],
                OB=[Buf(ob[:, l], "OB%d_%d" % (i, l), dgroup="S%d" % i) for l in range(L)]))
        ft = sb("FP", [128, NF, TN], F32)
        self.F = Pool([Buf(ft[:, i, :], "F%d" % i, dgroup="F") for i in range(NF)], "F")
        ht = sb("HP", [128, NH, TN], BF16)
        self.H = Pool([Buf(ht[:, i, :], "H%d" % i) for i in range(NH)], "H")
        wt = sb("WS", [128, n_wslots, 4096], BF16)
        wslots = [Buf(wt[:, i, :], "W%d" % i) for i in range(n_wslots)]
        self.W = WStream(fw, wslots, q="sp")
        self.DGS = wslots[:2]
        self.C = Buf(sb("C", [128, NCONST], F32)[:], "C")
        self.C8 = Buf(sb("C8", [128, L, 8], F32)[:], "C8")
        self.C16 = Buf(sb("C16", [128, L, 8], F32)[:], "C16")
        self.CH = self.C
        self.IDENT = Buf(sb("IDENT", [128, 128], F32)[:], "IDENT")
        self.ONES = Buf(sb("ONES", [128, 128], BF16)[:], "ONES")
        self.ONES1K = Buf(sb("ONES1K", [128, 128], BF16)[:], "ONES1K")
        self.ONES512 = Buf(sb("ONES512", [128, 128], BF16)[:], "ONES512")
        self.deferred = {0: deque(), 1: deque()}
        self.use_ctx(0)
        self.use_seq(0)
        self.PS = Pool([Buf(nc.alloc_psum_tensor("ps%d" % i, [128, 512], F32)[:], "ps%d" % i, psum=True) for i in range(8)], "PS")
        self.sbuf_left = nc.sbuf_bytes_remaining

    def cc(self, name, j=0, w=1):
        o = COFF[name] + j
        return self.C.ap[:, o:o + w]

    def ch(self, name, j=0, w=1):
        o = COFF[name] + j
        return self.CH.ap[:, o:o + w]

    def mm(self, ps, out_ap, lhsT, rhs, start, stop, reads, signal=None):
        self.fw.op("pe", lambda e: e.matmul(out_ap, lhsT=lhsT, rhs=rhs, start=start, stop=stop),
                   reads=reads, writes=[ps], signal=stop if signal is None else signal)

    def group(self, ps, out_ap, items):
        n = len(items)
        for i, (lt, rh, rd) in enumerate(items):
            self.mm(ps, out_ap, lt, rh, i == 0, i == n - 1, rd)

    def act(self, out_b, out_ap, in_b, in_ap, func, scale=None, bias=None, extra=()):
        kw = {}
        rd = [in_b] + list(extra)
        if scale is not None:
            kw["scale"] = scale
        if bias is not None:
            kw["bias"] = bias
        self.fw.op("act", lambda e: e.activation(out=out_ap, in_=in_ap, func=func, **kw), reads=rd, writes=[out_b])

    def tt(self, out_b, out_ap, a_b, a_ap, b_b, b_ap, op, eng="dve"):
        self.fw.op(eng, lambda e: e.tensor_tensor(out=out_ap, in0=a_ap, in1=b_ap, op=op), reads=[a_b, b_b], writes=[out_b])

    def bg_step(self):
        self.bg_tick += 1
        if self.bg_casts and self.bg_tick % 6 == 0:
            self.bg_casts.popleft()[1]()

    def cp(self, out_b, out_ap, in_b, in_ap, eng="dve", extra_w=(), extra_r=()):
        self.fw.op(eng, lambda e: e.tensor_copy(out=out_ap, in_=in_ap), reads=[in_b] + list(extra_r),
                   writes=[out_b] + list(extra_w))
        if eng == "pool":
            self.bg_step()

    def phase0(self):
        fw, nc = self.fw, self.nc
        fw.dma("pool", self.C.ap, self.consts_d, writes=[self.C])
        self.bg_casts = deque()
        self.bg_tick = 0
        order = [("wk_t", l) for l in range(DEPTH)] + [("wv_m", l) for l in range(DEPTH)]
        order += [(k, l) for l in range(DEPTH) for k in WSHAPES if k not in ("wk_t", "wv_m")]
        for k, l in order:
            rows, Fdim = WSHAPES[k]
            per_l = rows // DEPTH
            src = self.win[k][l * per_l:(l + 1) * per_l, :]
            dst = self.wsc[k][l * per_l:(l + 1) * per_l, :]
            if Fdim > 2048:
                src = src.rearrange("r (a b) -> r a b", b=Fdim // 2)
                dst = dst.rearrange("r (a b) -> r a b", b=Fdim // 2)
            elif Fdim == 1024:
                src = src.rearrange("(r two) f -> r (two f)", two=2)
                dst = dst.rearrange("(r two) f -> r (two f)", two=2)
            th = (lambda dst=dst, src=src, k=k, l=l: fw.dma("pool", dst, src, writes=[self.wsc_buf[(k, l)]], max_inflight=1))
            if l == 0 or k in ("wk_t", "wv_m"):
                th()
            else:
                self.bg_casts.append((l, th))
        fw.op("dve", lambda e: e.memset(self.ONES.ap, 1.0), writes=[self.ONES])
        fw.op("dve", lambda e: e.memset(self.ONES1K.ap, 1.0 / 1024), writes=[self.ONES1K])
        fw.op("dve", lambda e: e.memset(self.ONES512.ap, 1.0 / 512), writes=[self.ONES512])
        fw.op("dve", lambda e: e.memset(self.IDENT.ap, 1.0), writes=[self.IDENT])
        fw.op("pool", lambda e: e.affine_select(out=self.IDENT.ap, in_=self.IDENT.ap, pattern=[[1, 128]],
                                                 compare_op=ALU.is_equal, fill=0.0, base=0, channel_multiplier=-1),
              reads=[self.IDENT], writes=[self.IDENT])
        for l in range(DEPTH):
            for nm, w in (("b_gate%d" % l, 24), ("lru_b_r%d" % l, 8), ("lru_b_i%d" % l, 8)):
                col = self.cc(nm, 0, w)
                fw.op("dve", lambda e, col=col: e.tensor_scalar(out=col, in0=col, scalar1=0.5, scalar2=None, op0=ALU.mult),
                      reads=[self.C], writes=[self.C])
        for l in range(DEPTH):
            lam = self.cc("lru_lambda%d" % l, 0, 8)
            c8 = self.C8.ap[:, l, :]
            self.act(self.C8, c8, self.C, lam, AF.Sigmoid)
            self.act(self.C8, c8, self.C8, c8, AF.Ln)
            self.fw.op("dve", lambda e, c8=c8: e.tensor_scalar(out=c8, in0=c8, scalar1=8.0, scalar2=None, op0=ALU.mult),
                       reads=[self.C8], writes=[self.C8])
            c16 = self.C16.ap[:, l, :]
            self.fw.op("dve", lambda e, c8=c8, c16=c16: e.tensor_scalar(out=c16, in0=c8, scalar1=0.5, scalar2=None, op0=ALU.mult),
                       reads=[self.C8], writes=[self.C16])
        n = 0
        for l in range(self.depth):
            st = self.DGS[n % 2]; n += 1
            stv = st.ap.rearrange("p (a b) -> p a b", a=32)
            for k in range(4):
                for c in range(8):
                    col = self.cc("conv_a_w%d" % l, k * 8 + c)
                    oap = stv[:, k * 8 + c, :]
                    fw.op("dve", lambda e, oap=oap, col=col: e.tensor_scalar(out=oap, in0=self.IDENT.ap, scalar1=col, scalar2=None, op0=ALU.mult),
                          reads=[self.IDENT, self.C], writes=[st])
            fw.dma("pool", self.da_sc[l * 128:(l + 1) * 128, :], st.ap,
                   reads=[st], writes=[self.da_buf[l]], sem_buf=self.da_buf[l])
            for c in range(4):
                st = self.DGS[n % 2]; n += 1
                stv = st.ap.rearrange("p (a b) -> p a b", a=32)
                for k in range(31):
                    col = self.cc("conv_b_w%d" % l, c * 31 + k)
                    oap = stv[:, k, :]
                    fw.op("dve", lambda e, oap=oap, col=col: e.tensor_scalar(out=oap, in0=self.IDENT.ap, scalar1=col, scalar2=None, op0=ALU.mult),
                          reads=[self.IDENT, self.C], writes=[st])
                r0 = (l * 4 + c) * 128
                fw.dma("pool", self.db_sc[r0:r0 + 128, :], st.ap[:, 0:31 * 128],
                       reads=[st], writes=[self.db_buf[l]], sem_buf=self.db_buf[l])

    def _blk(self, key, l, nper, o0, no):
        r0 = (l * nper + o0) * 128
        Fdim = WSHAPES[key][1]
        ap = self.wsc[key][r0:r0 + no * 128, :].rearrange("(o p) f -> p o f", p=128)
        return ap, no * Fdim

    def sched_kv(self, l):
        W = self.W
        ap, pp = self._blk("wk_t", l, 4, 0, 4)
        W.schedule("wk%d" % l, ap, pp, self.wsc_buf[("wk_t", l)])
        W.schedule("wv%d" % l, self.wsc["wv_m"][l * 128:(l + 1) * 128, :], 4096, self.wsc_buf[("wv_m", l)])

    def sched_layer(self, l, part):
        W = self.W

        def win(s):
            ap, pp = self._blk("w_in_t", l, 52, 4 * s, 4)
            W.schedule("win%d.%d" % (l, s), ap, pp, self.wsc_buf[("w_in_t", l)])

        def tiled(key, nm, nper, o0, no):
            ap, pp = self._blk(key, l, nper, o0, no)
            W.schedule("%s%d.%d" % (nm, l, o0), ap, pp, self.wsc_buf[(key, l)])
        if part == 2:
            for s in range(6):
                no = 4 if s < 5 else 2
                tiled("wg_t", "fg", NFF, 4 * s, no)
                tiled("wu_t", "fu", NFF, 4 * s, no)
            for o in range(8):
                tiled("wd_t", "fd", 8, o, 1)
            return
        for s in range(7):
            win(s)
        W.schedule("da%d" % l, self.da_sc[l * 128:(l + 1) * 128, :], 4096, self.da_buf[l])
        W.schedule("lru%d" % l, self.wsc["lru_w"][l * 128:(l + 1) * 128, :], 2048, self.wsc_buf[("lru_w", l)])
        for c in range(4):
            r0 = (l * 4 + c) * 128
            W.schedule("db%d.%d" % (l, c), self.db_sc[r0:r0 + 128, :], 31 * 128, self.db_buf[l])
        tiled("proj_c_t", "pc", 8, 0, 8)
        for g in range(2):
            win(11 + g)
        tiled("proj_b_t", "pb", 8, 0, 8)
        for g in range(2):
            win(9 + g)
        for g in range(2):
            win(7 + g)
            tiled("proj_a_t", "pa", 8, 4 * g, 4)
        for g in range(2):
            tiled("w_out_t", "wo", 8, 4 * g, 4)

    def rounds(self):
        nt = self.seq // self.TN
        rs = []
        for s0 in range(0, self.nseq, 2):
            seqs = list(range(s0, min(s0 + 2, self.nseq)))
            for t in range(nt):
                rs.append([("p", i, s, t) for i, s in enumerate(seqs)])
        if self.with_sample:
            rs.append([("s", 0, self.nseq, 0)])
        return rs

    def schedule_all(self):
        done_kv = set()
        for rnd in self.rounds():
            for kind, slot, s, t in rnd:
                if kind == "p" and s not in done_kv:
                    done_kv.add(s)
                    for l in range(self.depth):
                        self.sched_kv(l)
            for l in range(self.depth):
                for _ in rnd:
                    self.sched_layer(l, 1)
                for _ in rnd:
                    self.sched_layer(l, 2)

    def layernorm(self, src, nch, N, ones_b, gname, bname, func, dst, post=None, defer=False, on_done=None, defer_to=None, eps=LN_EPS):
        F, H, PS = self.F, self.H, self.PS
        pm = PS.get(); pq = PS.get()
        for c in range(nch):
            q = H.get(); self.act(q, q.ap[:, :N], src[c], src[c].ap[:, :N], AF.Square)
            b = H.get(); self.cp(b, b.ap[:, :N], src[c], src[c].ap[:, :N])
            self.mm(pm, pm.ap[:, :N], ones_b.ap, b.ap[:, :N], c == 0, c == nch - 1, [ones_b, b], signal=True)
            self.mm(pq, pq.ap[:, :N], ones_b.ap, q.ap[:, :N], c == 0, c == nch - 1, [ones_b, q], signal=True)
            H.put(q, b)
        t = F.get()
        th = []
        th.append(lambda: self.act(t, t.ap[:, :N], pm, pm.ap[:, :N], AF.Square))

        def _var():
            self.tt(t, t.ap[:, :N], pq, pq.ap[:, :N], t, t.ap[:, :N], ALU.subtract)
            PS.put(pq)
        th.append(_var)
        th.append(lambda: self.fw.op("dve", lambda e: e.tensor_scalar(out=t.ap[:, :N], in0=t.ap[:, :N], scalar1=0.0, scalar2=eps, op0=ALU.max, op1=ALU.add),
                                     reads=[t], writes=[t]))
        th.append(lambda: self.act(t, t.ap[:, :N], t, t.ap[:, :N], AF.Sqrt))
        th.append(lambda: self.fw.op("dve", lambda e: e.reciprocal(out=t.ap[:, :N], in_=t.ap[:, :N]), reads=[t], writes=[t]))
        for c in range(nch):
            def _n1(c=c):
                s = src[c]
                self.tt(s, s.ap[:, :N], s, s.ap[:, :N], pm, pm.ap[:, :N], ALU.subtract)
                self.tt(s, s.ap[:, :N], s, s.ap[:, :N], t, t.ap[:, :N], ALU.mult)

            def _n2(c=c):
                s = src[c]
                db, dap = dst(c)
                self.act(db, dap, s, s.ap[:, :N], func, scale=self.cc(gname, c), bias=self.cc(bname, c), extra=[self.C])
                if post is not None:
                    post(c)
            th.append(_n1)
            th.append(_n2)

        def _fin():
            PS.put(pm)
            F.put(t)
            if on_done is not None:
                on_done()
        th.append(_fin)
        if defer_to is not None:
            defer_to.extend(th)
        elif defer:
            self.deferred[self.cur_slot].extend(th)
        else:
            for f in th:
                f()

    def kv_phase(self, s):
        fw, F, H, PS, W = self.fw, self.F, self.H, self.PS, self.W
        mf = [F.get() for _ in range(4)]
        mb = [H.get() for _ in range(4)]
        for i in range(4):
            src = self.memT[s, i * 256:(i + 1) * 256, :].rearrange("(c p) m -> p c m", p=128)
            fw.dma("sp", mf[i].ap[:, :512].rearrange("p (c m) -> p c m", c=2), src, writes=[mf[i]])
            self.cp(mb[i], mb[i].ap[:, :512], mf[i], mf[i].ap[:, :512])
        F.put(*mf)
        self.chk("kv1")

        def memb(kc):
            return mb[kc // 2], mb[kc // 2].ap[:, (kc % 2) * 256:(kc % 2) * 256 + 256]
        for l in range(self.depth):
            wk = W.next("wk%d" % l)
            self.chk("kv2")
            wkv = wk.ap[:, :4096].rearrange("p (o f) -> p o f", o=4)
            for h in range(4):
                ps = PS.get()
                self.group(ps, ps.ap[:, :256], [(wkv[:, h, kc * 128:(kc + 1) * 128], memb(kc)[1], [wk, memb(kc)[0]]) for kc in range(8)])
                self.act(self.KT[l], self.KT[l].ap[:, h, :], ps, ps.ap[:, :256], AF.Copy)
                st = F.get()
                self.cp(st, st.ap[:, :256], ps, ps.ap[:, :256])
                PS.put(ps)
                fw.dma("sp", self.o_kT[s, l, h], st.ap[:, :256], reads=[st])
                F.put(st)
            W.release(wk)
            self.chk("kv3")
            wv = W.next("wv%d" % l)
            wvv = wv.ap[:, :4096].rearrange("p (k n) -> p k n", k=8)
            for mc in range(2):
                ps = PS.get()
                self.group(ps, ps.ap[:, :512], [(memb(kc)[1][:, mc * 128:(mc + 1) * 128], wvv[:, kc, :], [wv, memb(kc)[0]]) for kc in range(8)])
                self.act(self.VV[l], self.VV[l].ap[:, mc, :], ps, ps.ap[:, :512], AF.Copy)
                st = F.get()
                self.cp(st, st.ap[:, :512], ps, ps.ap[:, :512])
                PS.put(ps)
                fw.dma("sp", self.o_v[s, l, mc * 128:(mc + 1) * 128, :], st.ap[:, :512], reads=[st])
                F.put(st)
            W.release(wv)
        H.put(*mb)

    def layer(self, l, N, last, sidx, is_sample):
        fw, F, H, PS, W = self.fw, self.F, self.H, self.PS, self.W
        X, XB, XAH, UH = self.X, self.XB, self.XAH, self.UH
        HA, HB, HST, KT, VV, OA, OB = self.HA, self.HB, self.HST, self.KT, self.VV, self.OA, self.OB
        ta, tb = min(3, N), min(30, N)

        def wview(w, no, fdim):
            return w.ap[:, :no * fdim].rearrange("p (o f) -> p o f", o=no)

        def xgroup(ps, wv, oi, rhs_bufs, w, nk=8):
            self.group(ps, ps.ap[:, :N], [(wv[:, oi, kc * 128:(kc + 1) * 128], rhs_bufs[kc].ap[:, :N], [w, rhs_bufs[kc]]) for kc in range(nk)])

        fw.op("pool", lambda e: e.tensor_copy(out=self.XAH_t[:, :, 1:4], in_=HA[l].ap), reads=[HA[l]], writes=XAH)
        fw.op("pool", lambda e: e.tensor_copy(out=self.UH_t[:, :, 2:32], in_=HB[l].ap), reads=[HB[l]], writes=UH)
        for g in range(2):
            w = W.next("win%d.%d" % (l, g)); wv = wview(w, 4, 1024)
            for oi in range(4):
                c = 4 * g + oi
                ps = PS.get(); xgroup(ps, wv, oi, XB, w)
                self.act(XAH[c], XAH[c].ap[:, HA_OFF:HA_OFF + N], ps, ps.ap[:, :N], AF.Copy)
                if last:
                    self.act(OA[l], OA[l].ap[:, c, 3 - ta:3], ps, ps.ap[:, N - ta:N], AF.Copy)
                PS.put(ps)
                self.pump(2)
            W.release(w)
        YG = []
        for g in range(2):
            w = W.next("win%d.%d" % (l, 2 + g)); wv = wview(w, 4, 1024)
            for oi in range(4):
                ps = PS.get(); xgroup(ps, wv, oi, XB, w)
                y = F.get(); self.act(y, y.ap[:, :N], ps, ps.ap[:, :N], AF.Gelu_apprx_tanh); YG.append(y)
                PS.put(ps)
                self.pump(2)
            W.release(w)
        w1 = W.next("win%d.4" % l); w2 = W.next("win%d.5" % l)
        wv1, wv2 = wview(w1, 4, 1024), wview(w2, 4, 1024)
        for c in range(4):
            p1 = PS.get(); xgroup(p1, wv1, c, XB, w1)
            p2 = PS.get(); xgroup(p2, wv2, c, XB, w2)
            sg = F.get(); self.act(sg, sg.ap[:, :N], p2, p2.ap[:, :N], AF.Sigmoid)
            PS.put(p2)
            self.tt(UH[c], UH[c].ap[:, HB_OFF:HB_OFF + N], p1, p1.ap[:, :N], sg, sg.ap[:, :N], ALU.mult)
            if last:
                self.tt(OB[l], OB[l].ap[:, c, 30 - tb:30], p1, p1.ap[:, N - tb:N], sg, sg.ap[:, N - tb:N], ALU.mult)
            PS.put(p1); F.put(sg)
            self.pump(2)
        W.release(w1); W.release(w2)
        Q = []
        w = W.next("win%d.6" % l); wv = wview(w, 4, 1024)
        for h in range(4):
            ps = PS.get(); xgroup(ps, wv, h, XB, w)
            q = H.get(); self.cp(q, q.ap[:, :N], ps, ps.ap[:, :N]); Q.append(q)
            PS.put(ps)
        W.release(w)
        fw.op("pool", lambda e: e.tensor_copy(out=HA[l].ap, in_=self.XAH_t[:, :, N + 1:N + 4]), reads=XAH, writes=[HA[l]])
        fw.op("pool", lambda e: e.tensor_copy(out=HB[l].ap, in_=self.UH_t[:, :, N + 2:N + 32]), reads=UH, writes=[HB[l]])
        if last:
            fw.dma("act", self.o_ca[sidx, l], OA[l].ap, reads=[OA[l]])
            fw.dma("act", self.o_cb[sidx, l], OB[l].ap, reads=[OB[l]])

        self.chk("win")
        wd = W.next("da%d" % l); wdv = wd.ap[:, :4096].rearrange("p (a b) -> p a b", a=32)
        wl = W.next("lru%d" % l); wlv = wl.ap[:, :2048].rearrange("p (g n d) -> p g n d", g=2, n=8)
        HG = [None] * 8
        V = [None] * 4
        st = {}

        cb = {}

        def conv_b0(c):
            wdb = W.next("db%d.%d" % (l, c)); wdbv = wdb.ap[:, :31 * 128].rearrange("p (a b) -> p a b", a=31)
            pv = PS.get()
            for k in range(16):
                self.mm(pv, pv.ap[:, :N], wdbv[:, k, :], UH[c].ap[:, 2 + k:2 + k + N], k == 0, False, [wdb, UH[c]])
            cb[c] = (wdb, wdbv, pv)

        def conv_b1(c):
            wdb, wdbv, pv = cb.pop(c)
            for k in range(16, 31):
                self.mm(pv, pv.ap[:, :N], wdbv[:, k, :], UH[c].ap[:, 2 + k:2 + k + N], False, k == 30, [wdb, UH[c]])
            v = F.get(); self.act(v, v.ap[:, :N], pv, pv.ap[:, :N], AF.Identity, bias=self.cc("conv_b_b%d" % l, c), extra=[self.C]); V[c] = v
            PS.put(pv); W.release(wdb)

        def s1(c):
            pc = PS.get()
            self.group(pc, pc.ap[:, :N], [(wdv[:, k * 8 + c, :], XAH[c].ap[:, 1 + k:1 + k + N], [wd, XAH[c]]) for k in range(4)])
            xc = F.get(); self.act(xc, xc.ap[:, :N], pc, pc.ap[:, :N], AF.Identity, bias=self.cc("conv_a_b%d" % l, c), extra=[self.C])
            PS.put(pc)
            xcb = H.get(); self.cp(xcb, xcb.ap[:, :N], xc, xc.ap[:, :N])
            st[c] = {"xc": xc, "xcb": xcb}

        def s2_act(c):
            d = st[c]; xcb = d.pop("xcb")
            pr = PS.get(); self.group(pr, pr.ap[:, :N], [(wlv[:, 0, c, :], xcb.ap[:, :N], [wl, xcb])])
            pi = PS.get(); self.group(pi, pi.ap[:, :N], [(wlv[:, 1, c, :], xcb.ap[:, :N], [wl, xcb])])
            H.put(xcb)
            r = F.get(); self.act(r, r.ap[:, :N], pr, pr.ap[:, :N], AF.Tanh, scale=0.5, bias=self.ch("lru_b_r%d" % l, c), extra=[self.CH]); PS.put(pr)
            i = F.get(); self.act(i, i.ap[:, :N], pi, pi.ap[:, :N], AF.Tanh, scale=0.5, bias=self.ch("lru_b_i%d" % l, c), extra=[self.CH]); PS.put(pi)
            h8 = self.C16.ap[:, l, c:c + 1]
            a = F.get(); self.act(a, a.ap[:, :N], r, r.ap[:, :N], AF.Exp, scale=h8, bias=h8, extra=[self.C16])
            d.update(r=r, i=i, a=a)

        def s2_T(c):
            d = st[c]; r, a = d["r"], d["a"]
            self.tt(r, r.ap[:, :N], a, a.ap[:, :N], a, a.ap[:, :N], ALU.mult)

        def s2_S(c):
            r = st[c]["r"]
            self.act(r, r.ap[:, :N], r, r.ap[:, :N], AF.Sqrt, scale=-0.25, bias=0.25)

        def s2_ix(c):
            d = st[c]; i, xc = d["i"], d.pop("xc")
            fw.op("dve", lambda e: e.scalar_tensor_tensor(out=i.ap[:, :N], in0=i.ap[:, :N], scalar=1.0, in1=xc.ap[:, :N], op0=ALU.add, op1=ALU.mult),
                  reads=[i, xc], writes=[i])
            F.put(xc)

        def s2_dve(c):
            d = st.pop(c); r, i, a = d["r"], d["i"], d["a"]
            self.tt(i, i.ap[:, :N], i, i.ap[:, :N], r, r.ap[:, :N], ALU.mult); F.put(r)
            hh = F.get()
            hst = HST[l]
            fw.op("dve", lambda e: e.tensor_tensor_scan(out=hh.ap[:, :N], data0=a.ap[:, :N], data1=i.ap[:, :N], initial=hst.ap[:, c:c + 1], op0=ALU.mult, op1=ALU.add),
                  reads=[a, i, hst], writes=[hh])
            F.put(a); F.put(i)
            self.cp(hst, hst.ap[:, c:c + 1], hh, hh.ap[:, N - 1:N])
            hg = H.get(); self.tt(hg, hg.ap[:, :N], hh, hh.ap[:, :N], YG[c], YG[c].ap[:, :N], ALU.mult)
            HG[c] = hg
            F.put(hh); F.put(YG[c])
        s1(0)
        s1(1)
        OT = [None] * 4
        kt, vvb = KT[l], VV[l]
        cst = {}

        def c_scores(h):
            pss = [PS.get(), PS.get()]
            pt = [H.get(), H.get()]
            for mc in range(2):
                self.group(pss[mc], pss[mc].ap[:, :N], [(kt.ap[:, h, mc * 128:(mc + 1) * 128], Q[h].ap[:, :N], [kt, Q[h]])])
                self.act(pt[mc], pt[mc].ap[:, :N], pss[mc], pss[mc].ap[:, :N], AF.Exp, scale=SCALE)
                PS.put(pss[mc])
            H.put(Q[h])
            cst[h] = pt

        def c_pv(h):
            pt = cst.pop(h)
            psum_s = PS.get()
            self.group(psum_s, psum_s.ap[:, :N], [(self.ONES.ap, pt[mc].ap[:, :N], [self.ONES, pt[mc]]) for mc in range(2)])
            po = PS.get()
            self.group(po, po.ap[:, :N], [(vvb.ap[:, mc, h * 128:(h + 1) * 128], pt[mc].ap[:, :N], [vvb, pt[mc]]) for mc in range(2)])
            rs = F.get()
            fw.op("dve", lambda e: e.reciprocal(out=rs.ap[:, :N], in_=psum_s.ap[:, :N]), reads=[psum_s], writes=[rs])
            PS.put(psum_s)
            ot = H.get(); OT[h] = ot
            self.tt(ot, ot.ap[:, :N], po, po.ap[:, :N], rs, rs.ap[:, :N], ALU.mult)
            PS.put(po); F.put(rs); H.put(*pt)
        c_scores(0)
        for h in range(4):
            if h + 1 < 4:
                c_scores(h + 1)
            c_pv(h)

        conv_b0(0)
        conv_b1(0)
        for it in range(10):
            if it + 2 < 8:
                s1(it + 2)
            if it < 8:
                s2_act(it)
            if it % 2 == 0 and it >= 2:
                s2_S(it - 2)
                s2_S(it - 1)
            if it < 8:
                s2_T(it)
                s2_ix(it)
            if it % 2 == 0 and it >= 2:
                s2_dve(it - 2)
                s2_dve(it - 1)
            if it < 6:
                if it % 2 == 0:
                    conv_b0(1 + it // 2)
                else:
                    conv_b1(1 + it // 2)
        W.release(wd); W.release(wl)
        if last:
            fw.dma("act", self.o_lru[sidx, l], HST[l].ap, reads=[HST[l]])
        self.chk("brA")

        YB = [None] * 4

        def dst_b(c):
            YB[c] = H.get()
            return YB[c], YB[c].ap[:, :N]
        self.chk("lnB")
        lnb = deque()
        self.layernorm(V, 4, N, self.ONES512, "ln_b_g%d" % l, "ln_b_b%d" % l, AF.Silu, dst_b, defer_to=lnb,
                       on_done=lambda: F.put(*V))

        def pump_lnb(n):
            while n > 0 and lnb:
                lnb.popleft()()
                n -= 1
        pump_lnb(2)

        M = [None] * 8
        MB = [None] * 8

        def gated_proj(gslot0, k, pname, nk, rhs_bufs, combine):
            if nk == 8:
                for g in range(2):
                    wg = W.next("win%d.%d" % (l, gslot0 + g)); wgv = wview(wg, 4, 1024)
                    wp = W.next("%s%d.%d" % (pname, l, 4 * g)); wpv = wview(wp, 4, 1024)
                    for oi in range(4):
                        o = 4 * g + oi
                        pg = PS.get(); xgroup(pg, wgv, oi, XB, wg)
                        po = PS.get(); xgroup(po, wpv, oi, rhs_bufs, wp)
                        G = F.get(); self.act(G, G.ap[:, :N], pg, pg.ap[:, :N], AF.Tanh, scale=0.5, bias=self.ch("b_gate%d" % l, k * 8 + o), extra=[self.CH]); PS.put(pg)
                        combine(o, G, po)
                        PS.put(po); F.put(G)
                        pump_lnb(2)
                    W.release(wg); W.release(wp)
            else:
                wp = W.next("%s%d.0" % (pname, l)); wpv = wview(wp, 8, 512)
                for g in range(2):
                    wg = W.next("win%d.%d" % (l, gslot0 + g)); wgv = wview(wg, 4, 1024)
                    for oi in range(4):
                        o = 4 * g + oi
                        pg = PS.get(); xgroup(pg, wgv, oi, XB, wg)
                        po = PS.get(); xgroup(po, wpv, o, rhs_bufs, wp, nk=4)
                        G = F.get(); self.act(G, G.ap[:, :N], pg, pg.ap[:, :N], AF.Tanh, scale=0.5, bias=self.ch("b_gate%d" % l, k * 8 + o), extra=[self.CH]); PS.put(pg)
                        combine(o, G, po)
                        PS.put(po); F.put(G)
                        pump_lnb(2)
                    W.release(wg)
                W.release(wp)

        def comb_a(o, G, po):
            m = F.get(); M[o] = m
            self.tt(m, m.ap[:, :N], po, po.ap[:, :N], G, G.ap[:, :N], ALU.mult)

        def comb_b(o, G, po):
            t = F.get()
            self.tt(t, t.ap[:, :N], po, po.ap[:, :N], G, G.ap[:, :N], ALU.mult)
            self.tt(M[o], M[o].ap[:, :N], M[o], M[o].ap[:, :N], t, t.ap[:, :N], ALU.add)
            F.put(t)

        def comb_c(o, G, po):
            t = F.get()
            self.tt(t, t.ap[:, :N], po, po.ap[:, :N], G, G.ap[:, :N], ALU.mult)
            mb = H.get(); MB[o] = mb
            self.tt(mb, mb.ap[:, :N], M[o], M[o].ap[:, :N], t, t.ap[:, :N], ALU.add)
            F.put(t); F.put(M[o])
        def gmul(out_b, G, po):
            fw.op("dve", lambda e: e.scalar_tensor_tensor(out=out_b.ap[:, :N], in0=G.ap[:, :N], scalar=1.0, in1=po.ap[:, :N], op0=ALU.add, op1=ALU.mult),
                  reads=[G, po], writes=[out_b])

        def comb_first(o, G, po):
            m = F.get(); M[o] = m
            gmul(m, G, po)

        def comb_mid(o, G, po):
            t = F.get()
            gmul(t, G, po)
            self.tt(M[o], M[o].ap[:, :N], M[o], M[o].ap[:, :N], t, t.ap[:, :N], ALU.add)
            F.put(t)

        def comb_last(o, G, po):
            t = F.get()
            gmul(t, G, po)
            mb = H.get(); MB[o] = mb
            self.tt(mb, mb.ap[:, :N], M[o], M[o].ap[:, :N], t, t.ap[:, :N], ALU.add)
            F.put(t); F.put(M[o])
        self.chk("projC")
        gated_proj(11, 2, "pc", 4, OT, comb_first)
        H.put(*OT)
        pump_lnb(100)
        self.chk("projB")
        gated_proj(9, 1, "pb", 4, YB, comb_mid)
        H.put(*YB)
        self.chk("projA")
        gated_proj(7, 0, "pa", 8, HG, comb_last)
        H.put(*HG)
        self.chk("wout")

        R1 = [None] * 8

        def resid(o, po, alpha=ALPHA):
            r1 = F.get(); R1[o] = r1
            fw.op("dve", lambda e: e.scalar_tensor_tensor(out=r1.ap[:, :N], in0=X[o].ap[:, :N], scalar=alpha, in1=po.ap[:, :N], op0=ALU.mult, op1=ALU.add),
                  reads=[X[o], po], writes=[r1])
        for g in range(2):
            w = W.next("wo%d.%d" % (l, 4 * g)); wv = wview(w, 4, 1024)
            for oi in range(4):
                o = 4 * g + oi
                po = PS.get(); xgroup(po, wv, oi, MB, w)
                resid(o, po, 2.0 * ALPHA)
                PS.put(po)
            W.release(w)
        H.put(*MB)

        def dst_x(c):
            return X[c], X[c].ap[:, :N]

        def post_x(c):
            self.cp(XB[c], XB[c].ap[:, :N], X[c], X[c].ap[:, :N], eng="pool")
        self.chk("ln1_start")
        r1a = list(R1)
        self.layernorm(r1a, 8, N, self.ONES1K, "ln1_g%d" % l, "ln1_b%d" % l, AF.Identity, dst_x, post_x,
                       defer=True, on_done=lambda: F.put(*r1a), eps=4.0 * LN_EPS)

        yield
        self.chk("ln1")
        HF = [None] * NFF
        for s in range(6):
            no = 4 if s < 5 else 2
            wg = W.next("fg%d.%d" % (l, 4 * s)); wgv = wview(wg, no, 1024)
            wu = W.next("fu%d.%d" % (l, 4 * s)); wuv = wview(wu, no, 1024)
            for oi in range(no):
                j = 4 * s + oi
                pg = PS.get(); xgroup(pg, wgv, oi, XB, wg)
                pu = PS.get(); xgroup(pu, wuv, oi, XB, wu)
                sg = F.get(); self.act(sg, sg.ap[:, :N], pg, pg.ap[:, :N], AF.Silu); PS.put(pg)
                hf = H.get(); HF[j] = hf
                self.tt(hf, hf.ap[:, :N], pu, pu.ap[:, :N], sg, sg.ap[:, :N], ALU.mult)
                PS.put(pu); F.put(sg)
                self.pump(2)
            W.release(wg); W.release(wu)
        self.chk("ffn_down")
        for o in range(8):
            w = W.next("fd%d.%d" % (l, o)); wv = w.ap[:, :2816].rearrange("p (o f) -> p o f", o=1)
            po = PS.get()
            self.group(po, po.ap[:, :N], [(wv[:, 0, kc * 128:(kc + 1) * 128], HF[kc].ap[:, :N], [w, HF[kc]]) for kc in range(NFF)])
            resid(o, po)
            PS.put(po)
            self.pump(2)
            W.release(w)
        H.put(*HF)
        self.chk("ln2")
        final = (l == self.depth - 1)
        r1b = list(R1)
        self.layernorm(r1b, 8, N, self.ONES1K, "ln2_g%d" % l, "ln2_b%d" % l, AF.Identity, dst_x, None if final else post_x,
                       defer=True, on_done=lambda: F.put(*r1b))

    def tile_begin(self, slot, src_ap, N, deferred=False):
        fw = self.fw
        X, XB, X_t = self.ctx_bufs[slot]
        dq = self.deferred[slot]

        def dst_x(c):
            return X[c], X[c].ap[:, :N]

        def post_x(c):
            self.cp(XB[c], XB[c].ap[:, :N], X[c], X[c].ap[:, :N], eng="pool")

        def load():
            fw.dma("sp", X_t[:, :, :N], src_ap.rearrange("(c p) n -> p c n", p=128), writes=X)

        def ln1():
            self.layernorm(X, 8, N, self.ONES1K, "ln_in_g", "ln_in_b", AF.Identity, dst_x, post_x, defer_to=dq)
        if deferred:
            dq.append(load)
            dq.append(ln1)
        else:
            load()
            ln1()
            self.flush(slot)

    def tile_end(self, slot, dst_ap, N, deferred=False):
        X, XB, X_t = self.ctx_bufs[slot]

        def store():
            self.fw.dma("act", dst_ap.rearrange("(c p) n -> p c n", p=128), X_t[:, :, :N], reads=X)
        if deferred:
            self.deferred[slot].append(store)
        else:
            store()

    def build(self):
        fw = self.fw
        try:
            self._build()
        except StopBuild:
            pass
        fw.wait_all_dma("pool")
        fw.finish()
        return self.nc

    def _build(self):
        fw = self.fw
        self.phase0()
        self.chk("phase0")
        self.schedule_all()
        TN = self.TN
        done_kv = set()
        rounds = self.rounds()
        begun = set()

        def tile_io(kind, s, t):
            if kind == "p":
                return self.xT[s, :, t * TN:(t + 1) * TN], self.yT[s, :, t * TN:(t + 1) * TN], TN, (t == self.seq // TN - 1)
            return self.xsT, self.ysT, 16, True
        for ri, rnd in enumerate(rounds):
            infos = []
            for kind, slot, s, t in rnd:
                self.use_ctx(slot)
                self.use_seq(slot)
                src, dst, N, last = tile_io(kind, s, t)
                if kind == "p":
                    if s not in done_kv:
                        done_kv.add(s)
                        for l in range(self.depth):
                            fw.op("pool", lambda e, b=self.HA[l]: e.memset(b.ap, 0.0), writes=[self.HA[l]])
                            fw.op("pool", lambda e, b=self.HB[l]: e.memset(b.ap, 0.0), writes=[self.HB[l]])
                            fw.op("pool", lambda e, b=self.HST[l]: e.memset(b.ap, 0.0), writes=[self.HST[l]])
                        self.chk("kv0")
                        self.kv_phase(s)
                        self.chk("kv")
                else:
                    self.flush(slot)
                    smp = Buf(None, "smp", dgroup="SMP")
                    for l in range(self.depth):
                        fw.dma("pool", self.HA[l].ap, self.sca[:, l], writes=[self.HA[l]], sem_buf=smp)
                        fw.dma("pool", self.HB[l].ap, self.scb[:, l], writes=[self.HB[l]], sem_buf=smp)
                        fw.dma("pool", self.HST[l].ap, self.slru[:, l], writes=[self.HST[l]], sem_buf=smp)
                        fw.dma("pool", self.KT[l].ap, self.ckT[:, l], writes=[self.KT[l]], sem_buf=smp)
                        fw.dma("pool", self.VV[l].ap, self.cv[:, l], writes=[self.VV[l]], sem_buf=smp)
                        fw.dma("pool", self.OB[l].ap[:, :, 0:14], self.scb[:, l, :, 16:30], writes=[self.OB[l]], sem_buf=smp)
                if (ri, slot) not in begun:
                    self.flush(slot)
                    self.tile_begin(slot, src, N)
                infos.append((slot, N, last, s, kind == "s", dst))
            nxt = {sl: (k2, s2, t2) for (k2, sl, s2, t2) in rounds[ri + 1]} if ri + 1 < len(rounds) else {}
            for l in range(self.depth):
                gens = []
                for slot, N, last, s, is_s, dst in infos:
                    self.use_ctx(slot)
                    self.use_seq(slot)
                    self.chk("xa")
                    self.flush(slot)
                    g = self.layer(l, N, last, s, is_s)
                    next(g)
                    gens.append((slot, g))
                while self.bg_casts and self.bg_casts[0][0] <= l + 1:
                    self.bg_casts.popleft()[1]()
                for gi, (slot, g) in enumerate(gens):
                    self.use_ctx(slot)
                    self.use_seq(slot)
                    self.flush(slot)
                    for _ in g:
                        pass
                    if l == self.depth - 1:
                        slot_, N_, last_, s_, is_s_, dst_ = infos[gi]
                        self.tile_end(slot_, dst_, N_, deferred=True)
                        if slot_ in nxt and nxt[slot_][0] == "p":
                            k2, s2, t2 = nxt[slot_]
                            src2, _, N2, _ = tile_io(k2, s2, t2)
                            self.tile_begin(slot_, src2, N2, deferred=True)
                            begun.add((ri + 1, slot_))
            self.chk("tile_out")
        for slot in list(self.deferred):
            self.flush(slot)


def assemble(results, nseq, seq, with_sample=True):
    n = len(results)
    L = DEPTH
    y = np.concatenate([r["yT"].transpose(0, 2, 1) for r in results], axis=0)
    ys = np.stack([r["ysT"].T for r in results], axis=0)

    def ca(idx):
        a = np.concatenate([r["o_ca"][idx] for r in results], axis=0)
        return np.ascontiguousarray(a.transpose(1, 0, 4, 3, 2)).reshape(L, a.shape[0], 3, 1024)

    def lru(idx):
        a = np.concatenate([r["o_lru"][idx] for r in results], axis=0)
        return np.ascontiguousarray(a.transpose(1, 0, 3, 2)).reshape(L, a.shape[0], 1024)

    def cb(idx):
        a = np.concatenate([r["o_cb"][idx] for r in results], axis=0)
        return np.ascontiguousarray(a.transpose(1, 0, 4, 3, 2)).reshape(L, a.shape[0], 30, 512)
    pi = slice(0, nseq)
    si = slice(nseq, nseq + 1)
    kT = np.concatenate([r["o_kT"] for r in results], axis=0)
    mk = np.ascontiguousarray(kT.transpose(1, 0, 4, 2, 3))
    v = np.concatenate([r["o_v"] for r in results], axis=0)
    mv = np.ascontiguousarray(v.transpose(1, 0, 2, 3)).reshape(L, v.shape[0], N_MEM, 4, 128)
    return (y, ys, ca(pi), lru(pi), cb(pi), mk, mv, ca(si), lru(si), cb(si))


_CACHE = {}


def kernel(**inputs):
    n = 8
    nseq, seq = 2, 4096
    inp = {k: np.asarray(v) for k, v in inputs.items()}
    shared = prep_shared(inp)
    in_maps = []
    for i in range(n):
        m = prep_core(inp, i, nseq, seq)
        m.update(shared)
        in_maps.append(m)
    nc = Builder(nseq, seq).build()
    res = run_bass_kernel_spmd(nc, in_maps, core_ids=list(range(n)))
    return assemble(res.results, nseq, seq)
```
